# Optimizing a Trainium2 kernel written in Bass

```python
import math
import jax, jax.numpy as jnp
from jax import lax
import numpy as np

D_MODEL = 4096
BATCH = 4
SEQ = 4096
DEPTH = 4

N_A_LAYERS = DEPTH // 2
N_B_LAYERS = DEPTH - N_A_LAYERS
PLE_DIM = 256
SGU_EXPAND = 2
SGU_WIDTH = SGU_EXPAND * D_MODEL
SGU_GROUPS = 16
SGU_GROUP_DIM = SGU_WIDTH // SGU_GROUPS
CHUNK = 128
DIFF_HEADS = 16
DIFF_HEAD_DIM = D_MODEL // DIFF_HEADS // 2
DIFF_V_DIM = 2 * DIFF_HEAD_DIM
DIFF_QK_WIDTH = DIFF_HEADS * 2 * DIFF_HEAD_DIM
DIFF_WIDTH = DIFF_HEADS * DIFF_V_DIM
Q_BLOCK = 128
RMS_EPS = 1e-6
LN_EPS = 1e-5
NEG_INF = -1e30

kernel_name = "yoco_sgu_diffattn_hybrid"


def rms_norm(x, g):
    xf = x.astype(jnp.float32)
    y = xf * lax.rsqrt(jnp.mean(xf * xf, axis=-1, keepdims=True) + RMS_EPS)
    return (y * g.astype(jnp.float32)).astype(x.dtype)


def layer_norm(x, g, b):
    xf = x.astype(jnp.float32)
    mu = jnp.mean(xf, axis=-1, keepdims=True)
    var = jnp.mean(jnp.square(xf - mu), axis=-1, keepdims=True)
    y = (xf - mu) * lax.rsqrt(var + LN_EPS)
    return (y * g.astype(jnp.float32) + b.astype(jnp.float32)).astype(x.dtype)


def lambda_init_fn(layer_idx):
    return 0.8 - 0.6 * math.exp(-0.3 * layer_idx)


def sgu_mixer(h, w_in, ln_g, ln_b, w_s, b_s, w_out):
    bsz, seq, _ = h.shape
    z = h @ w_in
    u, v, gate = jnp.split(z, 3, axis=-1)
    u = jax.nn.gelu(u)
    v = layer_norm(jax.nn.gelu(v), ln_g, ln_b)
    v = v.reshape(bsz, seq // CHUNK, CHUNK, SGU_GROUPS, SGU_GROUP_DIM)
    w_causal = w_s * jnp.tril(jnp.ones((CHUNK, CHUNK), w_s.dtype))
    mixed = jnp.einsum('gij,bnjgc->bnigc', w_causal, v) + b_s.T[:, :, None]
    mixed = mixed.reshape(bsz, seq, SGU_WIDTH)
    y = u * mixed * jax.nn.silu(gate)
    return y @ w_out


def shared_kv(h, kv_norm_g, w_kv, k_norm_g):
    bsz, seq, _ = h.shape
    kv = rms_norm(h, kv_norm_g) @ w_kv
    k, v = jnp.split(kv, [DIFF_QK_WIDTH], axis=-1)
    k = rms_norm(k.reshape(bsz, seq, DIFF_HEADS, 2, DIFF_HEAD_DIM), k_norm_g)
    v = v.reshape(bsz, seq, DIFF_HEADS, DIFF_V_DIM)
    return k, v


def diff_attn_mixer(h, k, v, w_in, q_norm_g, lam_q1, lam_k1, lam_q2, lam_k2, sub_g, w_out, lambda_init):
    bsz, seq, _ = h.shape
    z = h @ w_in
    q, gate = jnp.split(z, [DIFF_QK_WIDTH], axis=-1)
    q = rms_norm(q.reshape(bsz, seq, DIFF_HEADS, 2, DIFF_HEAD_DIM), q_norm_g)
    f32 = jnp.float32
    lam = (jnp.exp(jnp.sum(lam_q1.astype(f32) * lam_k1.astype(f32)))
           - jnp.exp(jnp.sum(lam_q2.astype(f32) * lam_k2.astype(f32))) + lambda_init)
    n_blk = seq // Q_BLOCK
    q_blocks = q.reshape(bsz, n_blk, Q_BLOCK, DIFF_HEADS, 2, DIFF_HEAD_DIM).transpose(1, 0, 2, 3, 4, 5)
    scale = DIFF_HEAD_DIM ** -0.5
    k_pos = jnp.arange(seq)

    def one_block(args):
        qb, blk = args
        s = jnp.einsum('bqhcd,bkhcd->bhcqk', qb, k).astype(f32) * scale
        q_pos = blk * Q_BLOCK + jnp.arange(Q_BLOCK)
        mask = k_pos[None, :] <= q_pos[:, None]
        s = jnp.where(mask, s, NEG_INF)
        a = jax.nn.softmax(s, axis=-1)
        w = a[:, :, 0] - lam * a[:, :, 1]
        return jnp.einsum('bhqk,bkhd->bqhd', w.astype(v.dtype), v)

    o = lax.map(one_block, (q_blocks, jnp.arange(n_blk)))
    o = o.transpose(1, 0, 2, 3, 4).reshape(bsz, seq, DIFF_HEADS, DIFF_V_DIM)
    o = rms_norm(o, sub_g) * (1.0 - lambda_init)
    o = o.reshape(bsz, seq, DIFF_WIDTH) * jax.nn.silu(gate)
    return o @ w_out


def per_layer_embed(h, p_i, w_ple, gate_g, gate_w):
    gate = jax.nn.sigmoid(rms_norm(h, gate_g) @ gate_w)
    return h + gate * (p_i @ w_ple)


def setup_inputs(seed: int = 0) -> dict:
    key = jax.random.key(seed)
    ks = jax.random.split(key, 24)
    f32 = jnp.float32
    D, E = D_MODEL, SGU_WIDTH
    nrm = lambda k, shape, s: jax.random.normal(k, shape, f32) * s
    gain = lambda k, shape: 1.0 + 0.02 * jax.random.normal(k, shape, f32)
    return {
        "x": nrm(ks[0], (BATCH, SEQ, D), 1.0),
        "p": nrm(ks[1], (DEPTH, BATCH, SEQ, PLE_DIM), 1.0),
        "a_norm_g": gain(ks[2], (N_A_LAYERS, D)),
        "a_w_in": nrm(ks[3], (N_A_LAYERS, D, 3 * E), D ** -0.5),
        "a_ln_g": gain(ks[4], (N_A_LAYERS, E)),
        "a_ln_b": nrm(ks[5], (N_A_LAYERS, E), 0.02),
        "a_w_s": nrm(ks[6], (N_A_LAYERS, SGU_GROUPS, CHUNK, CHUNK), CHUNK ** -0.5),
        "a_b_s": 1.0 + nrm(ks[7], (N_A_LAYERS, SGU_GROUPS, CHUNK), 0.1),
        "a_w_out": nrm(ks[8], (N_A_LAYERS, E, D), E ** -0.5),
        "kv_norm_g": gain(ks[9], (D,)),
        "w_kv": nrm(ks[10], (D, DIFF_QK_WIDTH + DIFF_WIDTH), D ** -0.5),
        "k_norm_g": gain(ks[11], (DIFF_HEAD_DIM,)),
        "b_norm_g": gain(ks[12], (N_B_LAYERS, D)),
        "b_w_in": nrm(ks[13], (N_B_LAYERS, D, DIFF_QK_WIDTH + DIFF_WIDTH), D ** -0.5),
        "b_q_norm_g": gain(ks[14], (N_B_LAYERS, DIFF_HEAD_DIM)),
        "b_lam_q1": nrm(ks[15], (N_B_LAYERS, DIFF_HEAD_DIM), 0.1),
        "b_lam_k1": nrm(ks[16], (N_B_LAYERS, DIFF_HEAD_DIM), 0.1),
        "b_lam_q2": nrm(ks[17], (N_B_LAYERS, DIFF_HEAD_DIM), 0.1),
        "b_lam_k2": nrm(ks[18], (N_B_LAYERS, DIFF_HEAD_DIM), 0.1),
        "b_sub_norm_g": gain(ks[19], (N_B_LAYERS, DIFF_V_DIM)),
        "b_w_out": nrm(ks[20], (N_B_LAYERS, DIFF_WIDTH, D), DIFF_WIDTH ** -0.5),
        "ple_w": nrm(ks[21], (DEPTH, PLE_DIM, D), PLE_DIM ** -0.5),
        "ple_gate_norm_g": gain(ks[22], (DEPTH, D)),
        "ple_gate_w": nrm(ks[23], (DEPTH, D, D), D ** -0.5),
    }


def reference(x, p, a_norm_g, a_w_in, a_ln_g, a_ln_b, a_w_s, a_b_s, a_w_out,
              kv_norm_g, w_kv, k_norm_g, b_norm_g, b_w_in, b_q_norm_g,
              b_lam_q1, b_lam_k1, b_lam_q2, b_lam_k2, b_sub_norm_g, b_w_out,
              ple_w, ple_gate_norm_g, ple_gate_w):
    h = x
    k_sh = v_sh = None
    for i in range(DEPTH):
        if i < N_A_LAYERS:
            h = h + sgu_mixer(rms_norm(h, a_norm_g[i]), a_w_in[i], a_ln_g[i], a_ln_b[i],
                              a_w_s[i], a_b_s[i], a_w_out[i])
        else:
            if i == N_A_LAYERS:
                k_sh, v_sh = shared_kv(h, kv_norm_g, w_kv, k_norm_g)
            j = i - N_A_LAYERS
            h = h + diff_attn_mixer(rms_norm(h, b_norm_g[j]), k_sh, v_sh, b_w_in[j], b_q_norm_g[j],
                                    b_lam_q1[j], b_lam_k1[j], b_lam_q2[j], b_lam_k2[j],
                                    b_sub_norm_g[j], b_w_out[j], lambda_init_fn(i))
        h = per_layer_embed(h, p[i], ple_w[i], ple_gate_norm_g[i], ple_gate_w[i])
    return h
```

```python
import math
import numpy as np
import ml_dtypes
import concourse.bass as bass
import concourse.mybir as mybir
from concourse.bass_utils import run_bass_kernel_spmd

F32 = mybir.dt.float32
BF16 = mybir.dt.bfloat16
AF = mybir.ActivationFunctionType
ALU = mybir.AluOpType
AX = mybir.AxisListType

ENGS = ("pe", "act", "dve", "pool", "sp")


class _Op:
    __slots__ = ("eng", "fn", "deps", "is_dma", "signal", "dsem", "dval", "dprev", "eidx")

    def __init__(self, eng, fn, deps, is_dma):
        self.eng = eng
        self.fn = fn
        self.deps = deps
        self.is_dma = is_dma
        self.signal = False
        self.dsem = None
        self.dval = 0
        self.dprev = None
        self.eidx = 0


class Prog:
    NDMA = 8
    NSIG = 8

    def __init__(self, nc):
        self.nc = nc
        self.ops = []
        self.lastw = {}
        self.readers = {}

    def op(self, eng, name, R, W, *a, **kw):
        is_dma = kw.pop("_dma", False)
        return self.add(eng, lambda e: getattr(e, name)(*a, **kw), R, W, is_dma)

    def add(self, eng, fn, R=(), W=(), is_dma=False):
        idx = len(self.ops)
        deps = set()
        R = list(R) + ["PHASE"]
        for t in R:
            w = self.lastw.get(t)
            if w is not None:
                deps.add(w)
        for t in W:
            w = self.lastw.get(t)
            if w is not None:
                deps.add(w)
            for r in self.readers.get(t, ()):
                deps.add(r)
        self.ops.append(_Op(eng, fn, deps, is_dma))
        for t in R:
            self.readers.setdefault(t, []).append(idx)
        for t in W:
            self.lastw[t] = idx
            self.readers[t] = []
        return idx

    def dma(self, q, out, in_, R=(), W=(), slow=False):
        if slow:
            return self.add(q, lambda e: e.dma_start(out=out, in_=in_, allow_slow_non_contiguous=True), R, W, is_dma=True)
        return self.add(q, lambda e: e.dma_start(out=out, in_=in_), R, W, is_dma=True)

    def barrier(self, tile):
        idx = len(self.ops)
        deps = set(self.readers.get("PHASE", ()))
        w = self.lastw.get("PHASE")
        if w is not None:
            deps.add(w)
        self.ops.append(_Op("pool", lambda e: e.memset(tile, 0.0), deps, False))
        self.lastw = {"PHASE": idx}
        self.readers = {"PHASE": []}

    def emit(self, final_wait_eng="sp"):
        nc = self.nc
        ops = self.ops
        per = {e: [] for e in ENGS}
        for i, op in enumerate(ops):
            op.eidx = len(per[op.eng])
            per[op.eng].append(i)
        for e in ENGS:
            waited = {f: -1 for f in ENGS}
            for i in per[e]:
                op = ops[i]
                best = {}
                keep = []
                for d in op.deps:
                    dop = ops[d]
                    if dop.is_dma:
                        keep.append(d)
                        continue
                    if dop.eng == e and e == "pe":
                        continue
                    if dop.eidx <= waited[dop.eng]:
                        continue
                    if dop.eng not in best or ops[best[dop.eng]].eidx < dop.eidx:
                        best[dop.eng] = d
                for f, d in best.items():
                    keep.append(d)
                    waited[f] = ops[d].eidx
                    ops[d].signal = True
                op.deps = sorted(keep)
        cm = []

        def mk(name):
            g = nc.semaphore(name)
            h = g.__enter__()
            cm.append(g)
            return h

        sig = {e: [mk(f"sg_{e}{k}") for k in range(self.NSIG)] for e in ENGS}
        dsm = {e: [mk(f"dm_{e}{k}") for k in range(self.NDMA)] for e in ("sp", "pool", "act")}
        cnt_sig = {e: 0 for e in ENGS}
        cnt_dma = {e: 0 for e in ENGS}
        last_dma_on_sem = {}
        for i, op in enumerate(ops):
            if op.is_dma:
                n = cnt_dma[op.eng]
                cnt_dma[op.eng] += 1
                k = n % self.NDMA
                op.dsem = dsm[op.eng][k]
                op.dval = 16 * (n // self.NDMA + 1)
                op.dprev = last_dma_on_sem.get((op.eng, k))
                last_dma_on_sem[(op.eng, k)] = i
            elif op.signal:
                n = cnt_sig[op.eng]
                cnt_sig[op.eng] += 1
                op.dsem = sig[op.eng][n % self.NSIG]
                op.dval = n // self.NSIG + 1
        self.stats = {e: (len(per[e]), cnt_sig[e], cnt_dma[e]) for e in ENGS}
        engattr = {"pe": "tensor", "act": "scalar", "dve": "vector", "pool": "gpsimd", "sp": "sync"}
        all_dma = [i for i, op in enumerate(ops) if op.is_dma]
        with nc.Block() as block:
            for e in ENGS:
                lst = per[e]
                if not lst and e != final_wait_eng:
                    continue

                def body(eng, e=e, lst=lst):
                    done = {}

                    def wait(sem, val):
                        k = id(sem)
                        if done.get(k, 0) >= val:
                            return
                        done[k] = val
                        eng.wait_ge(sem, val)

                    for i in lst:
                        op = ops[i]
                        if op.is_dma and op.dprev is not None:
                            p = ops[op.dprev]
                            wait(p.dsem, p.dval)
                        for d in op.deps:
                            dop = ops[d]
                            wait(dop.dsem, dop.dval)
                        ins = op.fn(eng)
                        if op.is_dma:
                            ins.then_inc(op.dsem, 16)
                        elif op.signal:
                            ins.then_inc(op.dsem, 1)
                    if e == final_wait_eng:
                        lastper = {}
                        for i in all_dma:
                            op = ops[i]
                            lastper[id(op.dsem)] = (op.dsem, max(op.dval, lastper.get(id(op.dsem), (None, 0))[1]))
                        for sem, val in lastper.values():
                            wait(sem, val)

                getattr(block, engattr[e])(body)
        for g in reversed(cm):
            g.__exit__(None, None, None)


class Cfg:
    def __init__(self, D=4096, NL=2048, DEPTH=4, PLE=256, B=4, NLA=None):
        self.B = B
        self.NLA = NLA or NL
        self.NSA = self.NLA // 128
        self.D = D
        self.NL = NL
        self.NS = NL // 128
        self.E = 2 * D
        self.G = 16
        self.GD = self.E // 16
        self.H = D // 256
        self.PLE = PLE
        self.KC = D // 128
        self.NA = DEPTH // 2
        self.NB = DEPTH - self.NA
        self.DEPTH = DEPTH
        self.TP = min(1024, NL)
        self.NTP = self.TP // 128


RMS_EPS = 1e-6
LN_EPS = 1e-5


def lambda_init_fn(layer_idx):
    return 0.8 - 0.6 * math.exp(-0.3 * layer_idx)


class Builder:
    def __init__(self, cfg, part):
        self.c = cfg
        self.part = part
        self.nc = bass.Bass("TRN2", target_bir_lowering=False)
        self.P = Prog(self.nc)
        self.rot = {}

    def din(self, name, shape, dt=F32):
        return self.nc.dram_tensor(name, list(shape), dt, kind="ExternalInput").ap()

    def dout(self, name, shape, dt=F32):
        return self.nc.dram_tensor(name, list(shape), dt, kind="ExternalOutput").ap()

    def dscr(self, name, shape, dt=F32):
        return self.nc.dram_tensor(name, list(shape), dt, kind="Internal").ap()

    def sb(self, name, shape, dt):
        self._uid = getattr(self, "_uid", 0) + 1
        return self.nc.sbuf_tensor(f"{name}_{self._uid}", shape, dt)

    def nxt(self, key, n):
        v = self.rot.get(key, 0)
        self.rot[key] = v + 1
        return v % n

    def build(self):
        c = self.c
        nc = self.nc
        P = self.P
        D, E, NL, NS, H, KC = c.D, c.E, c.NL, c.NS, c.H, c.KC
        part = self.part
        I = {}
        if part in ("A", "ALL"):
            I["x"] = self.din("x", [c.NLA, D])
            I["a_norm_g"] = self.din("a_norm_g", [c.NA, D])
            I["a_w_in"] = self.din("a_w_in", [c.NA, D, 3 * E])
            I["a_ln_g"] = self.din("a_ln_g", [c.NA, E])
            I["a_ln_b"] = self.din("a_ln_b", [c.NA, E])
            I["a_w_s"] = self.din("a_w_s", [c.NA, c.G, 128, 128])
            I["a_b_s"] = self.din("a_b_s", [c.NA, c.G, 128])
            I["a_w_out"] = self.din("a_w_out", [c.NA, E, D])
            I["kv_norm_g"] = self.din("kv_norm_g", [D])
            I["w_kv"] = self.din("w_kv", [D, 2 * D])
        I["k_norm_g"] = self.din("k_norm_g", [128])
        if part in ("B", "ALL"):
            I["b_norm_g"] = self.din("b_norm_g", [c.NB, D])
            I["b_w_in"] = self.din("b_w_in", [c.NB, D, 2 * D])
            I["b_q_norm_g"] = self.din("b_q_norm_g", [c.NB, 128])
            for nm in ("b_lam_q1", "b_lam_k1", "b_lam_q2", "b_lam_k2"):
                I[nm] = self.din(nm, [c.NB, 128])
            I["b_sub_norm_g"] = self.din("b_sub_norm_g", [c.NB, 256])
            I["b_w_out"] = self.din("b_w_out", [c.NB, D, D])
            I["masks"] = self.din("masks", [2, 2, 128, 128])
        nl_layers = {"A": c.NA, "B": c.NB, "ALL": c.DEPTH}[part]
        I["p"] = self.din("p", [nl_layers, NL if part == "B" else c.NLA, c.PLE])
        I["ple_w"] = self.din("ple_w", [nl_layers, c.PLE, D])
        I["ple_gate_norm_g"] = self.din("ple_gate_norm_g", [nl_layers, D])
        I["ple_gate_w"] = self.din("ple_gate_w", [nl_layers, D, D])
        self.I = I
        ktf = vf = kt_loc = v_loc = h_in = None
        if part == "A":
            h_fin = self.dout("h2", [c.NLA, D])
            kt_loc = self.dout("kt", [2 * H * 128, c.NLA], BF16)
            v_loc = self.dout("vv", [c.NLA, H * 256], BF16)
        elif part == "B":
            h_in = self.din("h2", [NL, D])
            ktf = self.din("ktf", [2, 2 * H * 128, NL], BF16)
            vf = self.din("vf", [2, NL, H * 256], BF16)
            h_fin = self.dout("out", [NL, D])
        else:
            h_fin = self.dout("out", [NL, D])
            kt_loc = self.dscr("kt", [2 * H * 128, c.NLA], BF16)
            v_loc = self.dscr("vv", [c.NLA, H * 256], BF16)
        S = {}
        S["hA"] = self.dscr("hA", [c.NLA, D])
        S["hB"] = self.dscr("hB", [c.NLA, D])
        if part in ("A", "ALL"):
            S["gv"] = self.dscr("gv", [c.NLA, E])
            S["qq"] = self.dscr("qq", [c.NLA, E])
            S["yT"] = self.dscr("yT", [c.NSA, 128, E // 128, 128], BF16)
        if part in ("B", "ALL"):
            S["qT"] = self.dscr("qT", [2 * H * 128, NL], BF16)
            S["sg"] = self.dscr("sg", [NL, D])
            S["oT"] = self.dscr("oT", [NS, 128, KC, 128], BF16)
        self.S = S

        with (
            nc.psum_tensor("pg", [128, 4, 512], F32) as pg,
            nc.psum_tensor("pm", [128, 2, 512], F32) as pm,
            nc.psum_tensor("pt", [128, 2, 1024], BF16) as pt,
            self.sb("ident", [128, 128], BF16) as ident,
            self.sb("identf", [128, 128], F32) as identf,
            self.sb("cst", [128, 48], F32) as cst,
            self.sb("bar", [128, 2], F32) as bar,
        ):
            self.pg, self.pm, self.pt, self.ident, self.cst, self.bar = pg, pm, pt, ident, cst, bar
            P.op("pool", "memset", [], ["identf"], identf[:], 0.0)
            P.op("pool", "affine_select", ["identf"], ["identf"], out=identf[:], in_=identf[:], pattern=[[-1, 128]],
                 compare_op=ALU.not_equal, fill=1.0, base=0, channel_multiplier=1)
            P.op("dve", "tensor_copy", ["identf"], ["ident"], out=ident[:], in_=identf[:])
            P.op("pool", "memset", [], ["cst"], cst[:, 0:1], RMS_EPS)
            P.op("pool", "memset", [], ["cst"], cst[:, 1:2], LN_EPS)
            P.op("pool", "memset", [], ["cst"], cst[:, 2:48], -0.5)
            self.barrier()

            if part in ("A", "ALL"):
                cur = I["x"]
                for l in range(c.NA):
                    self.sgu_layer(l, cur, S["hA"], c.NSA)
                    dst = h_fin if (part == "A" and l == c.NA - 1) else S["hB"]
                    self.ple_layer(l, S["hA"], dst, c.NSA)
                    cur = dst
                self.kv_phase(cur, kt_loc, v_loc, c.NSA)
                if part == "ALL":
                    assert c.NLA == 2 * NL
                    kview = lambda row0: kt_loc[row0:row0 + 128, :].rearrange("p (r n) -> p r n", r=2)
                    vview = lambda r, h: v_loc[r * NL:(r + 1) * NL, h * 256:(h + 1) * 256].rearrange("(b p) n -> p b n", p=128)
            else:
                cur = h_in
                kview = lambda row0: ktf[:, row0:row0 + 128, :].rearrange("r p n -> p r n")
                vview = lambda r, h: vf[r][:, h * 256:(h + 1) * 256].rearrange("(b p) n -> p b n", p=128)
            if part in ("B", "ALL"):
                for j in range(c.NB):
                    li = (c.NA + j) if part == "ALL" else j
                    self.attn_layer(j, c.NA + j, cur, S["hA"], kview, vview)
                    dst = S["hB"] if j < c.NB - 1 else h_fin
                    self.ple_layer(li, S["hA"], dst, NS)
                    cur = dst
            P.emit()
        return nc

    def barrier(self):
        self.P.barrier(self.bar[:, 0:1])

    def rsqrt_small(self, out, in_, scale, eps_col, n, R, W):
        P, cst = self.P, self.cst
        P.op("dve", "tensor_scalar", R, W, out=out, in0=in_, scalar1=scale, scalar2=cst[:, eps_col:eps_col + 1],
             op0=ALU.mult, op1=ALU.add)
        P.op("pool", "tensor_tensor", W, W, out=out, in0=out, in1=cst[:, 2:2 + n], op=ALU.pow)

    def norm_to_xT(self, src, t0, nt, g_ap, XT, KCx, bufs):
        c, P = self.c, self.P
        D = c.D
        hs_, hb_, gB, ssq, rs = bufs
        nbuf = hs_.shape[1] if len(hs_.shape) == 3 else 1
        P.dma("sp", gB[:], g_ap.partition_broadcast(128), W=["gB"])
        for ti in range(nt):
            t = t0 + ti
            b = ti % nbuf
            hs = hs_[:, b, :] if nbuf > 1 else hs_[:]
            nhb = hb_.shape[1] if len(hb_.shape) == 3 else 1
            bb = ti % nhb
            hb = hb_[:, bb, :] if nhb > 1 else hb_[:]
            P.dma("sp", hs, src[t * 128:(t + 1) * 128, :], W=[("hs", b)])
            P.op("act", "activation", [("hs", b)], [("hb", bb), ("ssq", ti)], out=hb, in_=hs, func=AF.Square,
                 accum_out=ssq[:, ti:ti + 1])
            self.rsqrt_small(rs[:, ti:ti + 1], ssq[:, ti:ti + 1], 1.0 / D, 0, 1, R=[("ssq", ti)], W=[("rs", ti)])
            P.op("dve", "scalar_tensor_tensor", [("hs", b), ("rs", ti), "gB"], [("hb", bb)], out=hb, in0=hs,
                 scalar=rs[:, ti:ti + 1], in1=gB[:], op0=ALU.mult, op1=ALU.mult)
            self.transpose_into(lambda kc, hb=hb: hb[:, kc * 128:(kc + 1) * 128], [("hb", bb)], KCx,
                                lambda kc0, n, ti=ti: XT[:, ti, kc0:kc0 + n, :], [("XT", ti)])

    def transpose_into(self, src_fn, Rtok, nblk, dst_fn, Wtok):
        P = self.P
        pt, ident = self.pt, self.ident
        k0 = 0
        while k0 < nblk:
            n = min(8, nblk - k0)
            pb = self.nxt("pt", 2)
            for j in range(n):
                P.op("pe", "transpose", list(Rtok) + ["ident"], [("pt", pb)], pt[:, pb, j * 128:(j + 1) * 128],
                     src_fn(k0 + j), ident[:])
            dst = dst_fn(k0, n)
            srcv = pt[:, pb, 0:n * 128].rearrange("p (a b) -> p a b", b=128)
            if self.nxt("cpeng", 2) == 0:
                P.op("act", "activation", [("pt", pb)], Wtok, out=dst, in_=srcv, func=AF.Copy)
            else:
                P.op("dve", "tensor_copy", [("pt", pb)], Wtok, out=dst, in_=srcv)
            k0 += n

    def gemm(self, XT, nt, slabs, epilogue, Wr, prologue=None, tail_depth=1):
        P = self.P
        pg = self.pg
        NW = Wr.shape[1]
        parts = [(si, ki, loads) for si, kparts in enumerate(slabs) for ki, loads in enumerate(kparts)]

        def load(pi):
            si, ki, loads = parts[pi]
            wb = self.nxt("wr", NW)
            kcn = None
            for (wap, coff) in loads:
                ncols = wap.shape[1]
                kcn = wap.shape[0] // 128
                P.dma("pool", Wr[:, wb, 0:kcn, coff:coff + ncols], wap.rearrange("(kc p) n -> p kc n", p=128),
                      W=[("W", wb)])
            return wb, kcn

        loaded = {0: load(0)}
        pending = []
        banks = None
        for pi, (si, ki, loads) in enumerate(parts):
            nk = len(slabs[si])
            if pi + 1 < len(parts):
                loaded[pi + 1] = load(pi + 1)
            wb, kcn = loaded.pop(pi)
            if ki == 0:
                banks = [self.nxt("pg", 4) for _ in range(nt)] if nk > 1 else None
            for ti in range(nt):
                bk = banks[ti] if banks else self.nxt("pg", 4)
                if prologue is not None and ki == 0:
                    prologue(si, ti)
                for kc in range(kcn):
                    P.op("pe", "matmul", [("W", wb), ("XT", ti)], [("pg", bk)],
                         pg[:, bk, :], XT[:, ti, ki * kcn + kc, :], Wr[:, wb, kc, :],
                         start=(ki == 0 and kc == 0), stop=(ki == nk - 1 and kc == kcn - 1))
                while len(pending) >= tail_depth:
                    pending.pop(0)()
                if ki == nk - 1:
                    tl = epilogue(si, ti, pg[:, bk, :], ("pg", bk))
                    if tl is not None:
                        pending.append(tl)
        while pending:
            pending.pop(0)()

    def sgu_layer(self, l, src, dst, NS):
        c, P, nc, I, S = self.c, self.P, self.nc, self.I, self.S
        D, E, NL, KC, GD = c.D, c.E, c.NL, c.KC, c.GD
        NTP = c.NTP
        w_in = I["a_w_in"][l]
        nvs = E // 512
        nqs = E // 256
        with self.sb("mv", [128, NS, 4], F32) as mv:
            with (
                self.sb("XT", [128, NTP, KC, 128], BF16) as XT,
                self.sb("Wr", [128, 2, 32, 512], BF16) as Wr,
                self.sb("hs", [128, 2, D], F32) as hs,
                self.sb("hb", [128, D], BF16) as hb,
                self.sb("gB", [128, D], F32) as gB,
                self.sb("ssq", [128, NTP], F32) as ssq,
                self.sb("rs", [128, NTP], F32) as rs,
                self.sb("vsum", [128, NS, nvs], F32) as vsum,
                self.sb("vsq", [128, NS, nvs], F32) as vsq,
                self.sb("e1", [128, 3, 512], F32) as e1,
                self.sb("e2", [128, 2, 512], F32) as e2,
            ):
                nb = (hs, hb, gB, ssq, rs)
                P.op("dve", "memset", [], ["vsum"], vsum[:], 0.0)
                P.op("dve", "memset", [], ["vsq"], vsq[:], 0.0)
                for t0 in range(0, NS, NTP):
                    self.norm_to_xT(src, t0, NTP, I["a_norm_g"][l], XT, KC, nb)
                    slabs = []
                    for s in range(nqs):
                        slabs.append([[(w_in[:, s * 256:(s + 1) * 256], 0),
                                       (w_in[:, 2 * E + s * 256:2 * E + (s + 1) * 256], 256)]])
                    for s in range(nvs):
                        slabs.append([[(w_in[:, E + s * 512:E + (s + 1) * 512], 0)]])

                    def epi(si, ti, ps, pstok, t0=t0):
                        t = t0 + ti
                        b = self.nxt("e1", 3)
                        if si < nqs:
                            P.op("act", "activation", [pstok], [("e1", b)], out=e1[:, b, 0:256], in_=ps[:, 0:256],
                                 func=AF.Gelu_apprx_tanh)
                            P.op("act", "activation", [pstok], [("e1", b)], out=e1[:, b, 256:512], in_=ps[:, 256:512],
                                 func=AF.Silu)
                            P.op("dve", "tensor_tensor", [("e1", b)], [("e1", b)], out=e1[:, b, 0:256], in0=e1[:, b, 0:256],
                                 in1=e1[:, b, 256:512], op=ALU.mult)
                            P.dma("sp", S["qq"][t * 128:(t + 1) * 128, si * 256:(si + 1) * 256], e1[:, b, 0:256],
                                  R=[("e1", b)], W=[("qq", t)])
                        else:
                            s = si - nqs
                            P.op("act", "activation", [pstok], [("e1", b), "vsum"], out=e1[:, b, :], in_=ps,
                                 func=AF.Gelu_apprx_tanh, accum_out=vsum[:, t, s:s + 1])
                            b2 = self.nxt("e2", 2)
                            P.op("dve", "scalar_tensor_tensor", [("e1", b)], [("e2", b2), "vsq"], out=e2[:, b2, :],
                                 in0=e1[:, b, :], in1=e1[:, b, :], scalar=1.0, op0=ALU.mult, op1=ALU.mult,
                                 accum_out=vsq[:, t, s:s + 1])
                            P.dma("sp", S["gv"][t * 128:(t + 1) * 128, s * 512:(s + 1) * 512], e1[:, b, :],
                                  R=[("e1", b)], W=[("gv", t)])

                    self.gemm(XT, NTP, slabs, epi, Wr)
                m0, m1, m2 = mv[:, :, 0], mv[:, :, 1], mv[:, :, 2]
                P.op("dve", "tensor_reduce", ["vsum"], ["mv"], out=m0, in_=vsum[:], axis=AX.X, op=ALU.add)
                P.op("dve", "tensor_reduce", ["vsq"], ["mv"], out=m1, in_=vsq[:], axis=AX.X, op=ALU.add)
                P.op("dve", "tensor_scalar", ["mv"], ["mv"], out=m0, in0=m0, scalar1=1.0 / E, scalar2=None, op0=ALU.mult)
                P.op("dve", "tensor_tensor", ["mv"], ["mv"], out=m2, in0=m0, in1=m0, op=ALU.mult)
                P.op("dve", "scalar_tensor_tensor", ["mv"], ["mv"], out=m1, in0=m1, scalar=1.0 / E, in1=m2,
                     op0=ALU.mult, op1=ALU.subtract)
                P.op("dve", "tensor_scalar", ["mv"], ["mv"], out=m1, in0=m1, scalar1=self.cst[:, 1:2], scalar2=None, op0=ALU.add)
                P.op("pool", "tensor_tensor", ["mv"], ["mv"], out=m1, in0=m1, in1=self.cst[:, 2:2 + NS], op=ALU.pow)
                self.barrier()

            EC = E // 128
            with (
                self.sb("lnG", [128, E], F32) as lnG,
                self.sb("lnB", [128, E], F32) as lnB,
                self.sb("gvs", [128, 2, E], F32) as gvs_,
                self.sb("vn", [128, E], BF16) as vn,
                self.sb("qs", [128, 4, 512], F32) as qs,
                self.sb("yb", [128, 3, 512], BF16) as yb,
                self.sb("yTs", [128, 2, EC, 128], BF16) as yTs,
                self.sb("wsf", [128, c.G, 128], F32) as wsf,
                self.sb("wsT", [128, c.G, 128], BF16) as wsT,
                self.sb("bsT", [128, c.G], F32) as bsT,
            ):
                P.dma("sp", lnG[:], I["a_ln_g"][l].partition_broadcast(128), W=["lnG"])
                P.dma("sp", lnB[:], I["a_ln_b"][l].partition_broadcast(128), W=["lnB"])
                P.dma("sp", wsf[:], I["a_w_s"][l].rearrange("g i j -> i g j"), W=["wsf"])
                P.op("pool", "affine_select", ["wsf"], ["wsf"], out=wsf[:], in_=wsf[:], pattern=[[0, c.G], [-1, 128]],
                     compare_op=ALU.is_ge, fill=0.0, base=0, channel_multiplier=1)
                P.op("dve", "tensor_copy", ["wsf"], [("yTs", 0)], out=yTs[:, 0, 0:c.G, :], in_=wsf[:])
                self.transpose_into(lambda g: yTs[:, 0, g, :], [("yTs", 0)], c.G,
                                    lambda g0, n: wsT[:, g0:g0 + n, :], ["wsT"])
                P.dma("sp", bsT[:], I["a_b_s"][l].rearrange("g i -> i g"), W=["bsT"], slow=True)
                ncol = min(GD, 512)
                npg = GD // ncol
                pend = None
                NCH = 4
                CW = E // NCH
                gtok = lambda gb: [("gvs", gb)] + [("gc", gb, ch) for ch in range(NCH)]
                P.dma("sp", gvs_[:, 0, :], S["gv"][0:128, :], R=[("gv", 0)], W=gtok(0))
                for t in range(NS):
                    gb = t % 2
                    gvs = gvs_[:, gb, :]
                    if t + 1 < NS:
                        P.dma("sp", gvs_[:, 1 - gb, :], S["gv"][(t + 1) * 128:(t + 2) * 128, :], R=[("gv", t + 1)],
                              W=gtok(1 - gb))
                    yt = t % 2

                    def ln_a(ch):
                        cs = slice(ch * CW, (ch + 1) * CW)
                        P.op("dve", "tensor_scalar", [("gvs", gb), "mv"], [("gc", gb, ch)], out=gvs[:, cs], in0=gvs[:, cs],
                             scalar1=mv[:, t, 0:1], scalar2=mv[:, t, 1:2], op0=ALU.subtract, op1=ALU.mult)
                        P.op("pool", "tensor_tensor", [("gc", gb, ch), "lnG"], [("gc", gb, ch)], out=gvs[:, cs], in0=gvs[:, cs],
                             in1=lnG[:, cs], op=ALU.mult)

                    def ln_b(ch):
                        cs = slice(ch * CW, (ch + 1) * CW)
                        P.op("dve", "tensor_tensor", [("gc", gb, ch), "lnB"], [("vn", ch)], out=vn[:, cs], in0=gvs[:, cs],
                             in1=lnB[:, cs], op=ALU.add)

                    def mix(ch):
                        nonlocal pend
                        for col0 in range(ch * CW, (ch + 1) * CW, ncol):
                            g = col0 // GD
                            mb = self.nxt("pm", 2)
                            P.op("pe", "matmul", ["wsT", ("vn", ch)], [("pm", mb)], self.pm[:, mb, 0:ncol], wsT[:, g, :],
                                 vn[:, col0:col0 + ncol], start=True, stop=True)
                            qb = self.nxt("qs", 4)
                            P.dma("sp", qs[:, qb, 0:ncol], S["qq"][t * 128:(t + 1) * 128, col0:col0 + ncol],
                                  R=[("qq", t)], W=[("qs", qb)])
                            ybb = self.nxt("yb", 3)
                            P.op("dve", "scalar_tensor_tensor", [("pm", mb), ("qs", qb), "bsT"], [("yb", ybb)],
                                 out=yb[:, ybb, 0:ncol], in0=self.pm[:, mb, 0:ncol], scalar=bsT[:, g:g + 1],
                                 in1=qs[:, qb, 0:ncol], op0=ALU.add, op1=ALU.mult)
                            ec0 = col0 // 128
                            if pend is not None:
                                pend()

                            def pend(ybb=ybb, ec0=ec0, yt=yt):
                                self.transpose_into(lambda k: yb[:, ybb, k * 128:(k + 1) * 128], [("yb", ybb)], ncol // 128,
                                                    lambda k0, n: yTs[:, yt, ec0 + k0:ec0 + k0 + n, :], [("yTs", yt)])

                    ln_a(0)
                    for ch in range(NCH):
                        if ch + 1 < NCH:
                            ln_a(ch + 1)
                        ln_b(ch)
                        mix(ch)
                    pend()
                    pend = None
                    P.dma("sp", S["yT"][t], yTs[:, yt, :, :], R=[("yTs", yt)], W=[("yTd", t)])
                self.barrier()

        self.out_proj(S["yT"], "yTd", E // 128, I["a_w_out"][l], src, dst, NS)

    def out_proj(self, xT_dram, xtok, KCx, w_ap, res_src, dst, NS):
        c, P, nc = self.c, self.P, self.nc
        D = c.D
        ntp = min(NS, (8 * 32) // KCx)
        kpc = min(32, KCx)
        nkp = KCx // kpc
        with (
            self.sb("XT", [128, ntp, KCx, 128], BF16) as XT,
            self.sb("Wr", [128, 2, 32, 512], BF16) as Wr,
            self.sb("hr", [128, 4, 512], F32) as hr,
        ):
            for t0 in range(0, NS, ntp):
                for ti in range(ntp):
                    P.dma("sp", XT[:, ti, :, :], xT_dram[t0 + ti], R=[(xtok, t0 + ti)], W=[("XT", ti)])
                slabs = []
                for s in range(D // 512):
                    slabs.append([[(w_ap[kp * kpc * 128:(kp + 1) * kpc * 128, s * 512:(s + 1) * 512], 0)] for kp in range(nkp)])

                hrb = {}

                def pro(si, ti, t0=t0):
                    t = t0 + ti
                    b = self.nxt("hr", 4)
                    hrb[(si, ti)] = b
                    P.dma("sp", hr[:, b, :], res_src[t * 128:(t + 1) * 128, si * 512:(si + 1) * 512], W=[("hr", b)])

                def epi(si, ti, ps, pstok, t0=t0):
                    t = t0 + ti
                    b = hrb.pop((si, ti))
                    P.op("dve", "tensor_tensor", [pstok, ("hr", b)], [("hr", b)], out=hr[:, b, :], in0=ps, in1=hr[:, b, :],
                         op=ALU.add)
                    P.dma("sp", dst[t * 128:(t + 1) * 128, si * 512:(si + 1) * 512], hr[:, b, :], R=[("hr", b)], W=[("hd", t)])

                self.gemm(XT, ntp, slabs, epi, Wr, prologue=pro)
            self.barrier()

    def ple_layer(self, li, src, dst, NS):
        c, P, nc, I = self.c, self.P, self.nc, self.I
        D, KC, NTP = c.D, c.KC, c.NTP
        PK = c.PLE // 128
        with (
            self.sb("XT", [128, NTP, KC, 128], BF16) as XT,
            self.sb("Wr", [128, 2, 32, 512], BF16) as Wr,
            self.sb("hs", [128, 2, D], F32) as hs,
            self.sb("hb", [128, D], BF16) as hb,
            self.sb("gB", [128, D], F32) as gB,
            self.sb("ssq", [128, NTP], F32) as ssq,
            self.sb("rs", [128, NTP], F32) as rs,
            self.sb("pf", [128, 2, c.PLE], F32) as pf,
            self.sb("pb", [128, 2, c.PLE], BF16) as pb,
            self.sb("pT", [128, NTP, PK, 128], BF16) as pT,
            self.sb("Wp", [128, 2, PK, 512], BF16) as Wp,
            self.sb("hr", [128, 3, 512], F32) as hr,
            self.sb("sgm", [128, 2, 512], F32) as sgm,
        ):
            nb = (hs, hb, gB, ssq, rs)
            for t0 in range(0, NS, NTP):
                self.norm_to_xT(src, t0, NTP, I["ple_gate_norm_g"][li], XT, KC, nb)
                for ti in range(NTP):
                    t = t0 + ti
                    b = ti % 2
                    P.dma("sp", pf[:, b, :], I["p"][li][t * 128:(t + 1) * 128, :], W=[("pf", b)])
                    P.op("dve", "tensor_copy", [("pf", b)], [("pb", b)], out=pb[:, b, :], in_=pf[:, b, :])
                    self.transpose_into(lambda k, b=b: pb[:, b, k * 128:(k + 1) * 128], [("pb", b)], PK,
                                        lambda k0, n, ti=ti: pT[:, ti, k0:k0 + n, :], [("pT", ti)])
                slabs = [[[(I["ple_gate_w"][li][:, s * 512:(s + 1) * 512], 0)]] for s in range(D // 512)]
                state = {}
                hrb = {}

                def pro(si, ti, t0=t0):
                    t = t0 + ti
                    if ti == 0:
                        def ldwp(s):
                            wpb_ = self.nxt("wp", 2)
                            state[("wpb", s)] = wpb_
                            P.dma("pool", Wp[:, wpb_, :, :],
                                  I["ple_w"][li][:, s * 512:(s + 1) * 512].rearrange("(kc p) n -> p kc n", p=128),
                                  W=[("Wp", wpb_)])
                        if si == 0:
                            ldwp(0)
                        if si + 1 < D // 512:
                            ldwp(si + 1)
                    b = self.nxt("hr", 3)
                    hrb[(si, ti)] = b
                    P.dma("sp", hr[:, b, :], src[t * 128:(t + 1) * 128, si * 512:(si + 1) * 512], W=[("hr", b)])

                def epi(si, ti, ps, pstok, t0=t0):
                    t = t0 + ti
                    wpb = state[("wpb", si)]
                    sb = self.nxt("sgm", 2)
                    P.op("act", "activation", [pstok], [("sgm", sb)], out=sgm[:, sb, :], in_=ps, func=AF.Sigmoid)
                    mb = self.nxt("pm", 2)
                    for k in range(PK):
                        P.op("pe", "matmul", [("pT", ti), ("Wp", wpb)], [("pm", mb)], self.pm[:, mb, :], pT[:, ti, k, :],
                             Wp[:, wpb, k, :], start=(k == 0), stop=(k == PK - 1))
                    b = hrb.pop((si, ti))
                    P.op("dve", "tensor_tensor", [("sgm", sb), ("pm", mb)], [("sgm", sb)], out=sgm[:, sb, :], in0=sgm[:, sb, :],
                         in1=self.pm[:, mb, :], op=ALU.mult)
                    P.op("pool", "tensor_tensor", [("sgm", sb), ("hr", b)], [("hr", b)], out=hr[:, b, :], in0=sgm[:, sb, :],
                         in1=hr[:, b, :], op=ALU.add)
                    P.dma("sp", dst[t * 128:(t + 1) * 128, si * 512:(si + 1) * 512], hr[:, b, :], R=[("hr", b)], W=[("hd", t)])

                self.gemm(XT, NTP, slabs, epi, Wr, prologue=pro)
            self.barrier()

    def qk_epilogue(self, ps, pstok, gq, qT_dram, row0, t, bufs):
        P = self.P
        sq, qn, qts, ss4 = bufs
        b = self.nxt("sq", 4)
        v3 = lambda ap: ap.rearrange("p (a d) -> p a d", d=128)
        P.op("act", "activation", [pstok], [("sq", b)], out=sq[:, b, :], in_=ps, func=AF.Square)
        P.op("dve", "tensor_reduce", [("sq", b)], [("ss4", b)], out=ss4[:, b, :], in_=v3(sq[:, b, :]), axis=AX.X, op=ALU.add)
        self.rsqrt_small(ss4[:, b, :], ss4[:, b, :], 1.0 / 128, 0, 4, R=[("ss4", b)], W=[("ss4", b)])
        P.op("dve", "tensor_tensor", [pstok, ("ss4", b)], [("sq", b)], out=v3(sq[:, b, :]), in0=v3(ps),
             in1=ss4[:, b, :].unsqueeze(2).broadcast_to([128, 4, 128]), op=ALU.mult)
        P.op("pool", "tensor_tensor", [("sq", b), "gq"], [("qn", b)], out=v3(qn[:, b, :]), in0=v3(sq[:, b, :]),
             in1=gq.unsqueeze(1).broadcast_to([128, 4, 128]), op=ALU.mult)
        def tail():
            self.transpose_into(lambda k: qn[:, b, k * 128:(k + 1) * 128], [("qn", b)], 4,
                                lambda k0, n: qts[:, b, k0:k0 + n, :], [("qts", b)])
            P.dma("sp", qT_dram[row0:row0 + 512, t * 128:(t + 1) * 128].rearrange("(s p) n -> p s n", p=128), qts[:, b, :, :],
                  R=[("qts", b)], W=[("qTd", t)])
        return tail

    def kv_phase(self, src, kt_loc, v_loc, NS):
        c, P, nc, I = self.c, self.P, self.nc, self.I
        D, KC, NTP = c.D, c.KC, c.NTP
        with (
            self.sb("XT", [128, NTP, KC, 128], BF16) as XT,
            self.sb("Wr", [128, 2, 32, 512], BF16) as Wr,
            self.sb("hs", [128, 2, D], F32) as hs,
            self.sb("hb", [128, D], BF16) as hb,
            self.sb("gB", [128, D], F32) as gB,
            self.sb("ssq", [128, NTP], F32) as ssq,
            self.sb("rs", [128, NTP], F32) as rs,
            self.sb("sq", [128, 4, 512], F32) as sq,
            self.sb("qn", [128, 4, 512], BF16) as qn,
            self.sb("qts", [128, 4, 4, 128], BF16) as qts,
            self.sb("ss4", [128, 4, 4], F32) as ss4,
            self.sb("gk", [128, 128], F32) as gk,
            self.sb("vs", [128, 3, 512], BF16) as vs,
        ):
            nb = (hs, hb, gB, ssq, rs)
            P.dma("sp", gk[:], I["k_norm_g"].partition_broadcast(128), W=["gq"])
            nks = D // 512
            for t0 in range(0, NS, NTP):
                self.norm_to_xT(src, t0, NTP, I["kv_norm_g"], XT, KC, nb)
                slabs = [[[(I["w_kv"][:, s * 512:(s + 1) * 512], 0)]] for s in range(2 * nks)]

                def epi(si, ti, ps, pstok, t0=t0):
                    t = t0 + ti
                    if si < nks:
                        return self.qk_epilogue(ps, pstok, gk[:], kt_loc, si * 512, t, (sq, qn, qts, ss4))
                    else:
                        b = self.nxt("vs", 3)
                        P.op("act", "activation", [pstok], [("vs", b)], out=vs[:, b, :], in_=ps, func=AF.Copy)
                        P.dma("sp", v_loc[t * 128:(t + 1) * 128, (si - nks) * 512:(si - nks + 1) * 512], vs[:, b, :],
                              R=[("vs", b)], W=[("vd", t)])

                self.gemm(XT, NTP, slabs, epi, Wr, tail_depth=2)
            self.barrier()

    def exchange(self, kt_loc, v_loc, ktf, vf):
        P = self.P
        groups = [[2 * i, 2 * i + 1] for i in range(self.c.B)]
        P.op("pool", "collective_compute", [], ["cc1"], "AllGather", ALU.bypass, replica_groups=groups, ins=[kt_loc],
             outs=[ktf], _dma=True)
        P.op("pool", "collective_compute", [], ["cc2"], "AllGather", ALU.bypass, replica_groups=groups, ins=[v_loc],
             outs=[vf], _dma=True)
        self.barrier()

    def attn_layer(self, j, layer_idx, src, dst, kview, vview):
        c, P, nc, I, S = self.c, self.P, self.nc, self.I, self.S
        D, NL, NS, KC, NTP, H = c.D, c.NL, c.NS, c.KC, c.NTP, c.H
        lam_init = lambda_init_fn(layer_idx)
        SCALE = 128 ** -0.5
        with (
            self.sb("XT", [128, NTP, KC, 128], BF16) as XT,
            self.sb("Wr", [128, 2, 32, 512], BF16) as Wr,
            self.sb("hs", [128, 2, D], F32) as hs,
            self.sb("hb", [128, D], BF16) as hb,
            self.sb("gB", [128, D], F32) as gB,
            self.sb("ssq", [128, NTP], F32) as ssq,
            self.sb("rs", [128, NTP], F32) as rs,
            self.sb("sq", [128, 4, 512], F32) as sq,
            self.sb("qn", [128, 4, 512], BF16) as qn,
            self.sb("qts", [128, 4, 4, 128], BF16) as qts,
            self.sb("ss4", [128, 4, 4], F32) as ss4,
            self.sb("gq", [128, 128], F32) as gq,
            self.sb("sgs", [128, 3, 512], F32) as sgs,
        ):
            nb = (hs, hb, gB, ssq, rs)
            P.dma("sp", gq[:], I["b_q_norm_g"][j].partition_broadcast(128), W=["gq"])
            nqs = D // 512
            for t0 in range(0, NS, NTP):
                self.norm_to_xT(src, t0, NTP, I["b_norm_g"][j], XT, KC, nb)
                slabs = [[[(I["b_w_in"][j][:, s * 512:(s + 1) * 512], 0)]] for s in range(2 * nqs)]

                def epi(si, ti, ps, pstok, t0=t0):
                    t = t0 + ti
                    if si < nqs:
                        return self.qk_epilogue(ps, pstok, gq[:], S["qT"], si * 512, t, (sq, qn, qts, ss4))
                    else:
                        b = self.nxt("sgs", 3)
                        P.op("act", "activation", [pstok], [("sgs", b)], out=sgs[:, b, :], in_=ps, func=AF.Silu)
                        P.dma("sp", S["sg"][t * 128:(t + 1) * 128, (si - nqs) * 512:(si - nqs + 1) * 512], sgs[:, b, :],
                              R=[("sgs", b)], W=[("sgd", t)])

                self.gemm(XT, NTP, slabs, epi, Wr, tail_depth=2)
            self.barrier()

        pg, pm = self.pg, self.pm
        with (
            self.sb("kts", [128, 2, 2, 2, NL], BF16) as kts,
            self.sb("vt", [128, 2, 2, NS, 258], BF16) as vt,
            self.sb("qt", [128, 3, 2, 128], BF16) as qt,
            self.sb("sgt", [128, 3, 256], F32) as sgt,
            self.sb("et", [128, 3, 512], BF16) as et,
            self.sb("maskb", [128, 2, 2, 128], BF16) as maskb,
            self.sb("lv", [128, 4, 128], F32) as lv,
            self.sb("lj", [128, 128], F32) as lj,
            self.sb("sc", [128, 16], F32) as sc,
            self.sb("gq2", [128, 2, 128], F32) as gq2,
            self.sb("subg", [128, 256], F32) as subg,
            self.sb("rr", [128, 3, 8], F32) as rr,
            self.sb("ot", [128, 3, 256], F32) as ot,
            self.sb("o2", [128, 3, 256], F32) as o2,
            self.sb("ob", [128, 3, 256], BF16) as ob,
            self.sb("ots", [128, 3, 2, 128], BF16) as ots,
        ):
            for k, nm in enumerate(("b_lam_q1", "b_lam_k1", "b_lam_q2", "b_lam_k2")):
                P.dma("sp", lv[:, k, :], I[nm][j].partition_broadcast(128), W=[("lv", k)])
            P.op("dve", "memset", [], ["sc"], sc[:], 0.0)
            P.op("dve", "scalar_tensor_tensor", [("lv", 0), ("lv", 1), "sc"], ["lj", "sc"], out=lj[:], in0=lv[:, 0, :],
                 in1=lv[:, 1, :], scalar=1.0, op0=ALU.mult, op1=ALU.mult, accum_out=sc[:, 0:1])
            P.op("dve", "scalar_tensor_tensor", [("lv", 2), ("lv", 3), "sc", "lj"], ["lj", "sc"], out=lj[:], in0=lv[:, 2, :],
                 in1=lv[:, 3, :], scalar=1.0, op0=ALU.mult, op1=ALU.mult, accum_out=sc[:, 1:2])
            P.op("act", "activation", ["sc"], ["sc"], out=sc[:, 2:4], in_=sc[:, 0:2], func=AF.Exp)
            P.op("dve", "tensor_tensor", ["sc"], ["sc"], out=sc[:, 4:5], in0=sc[:, 3:4], in1=sc[:, 2:3], op=ALU.subtract)
            P.op("dve", "tensor_scalar", ["sc"], ["sc"], out=sc[:, 4:5], in0=sc[:, 4:5], scalar1=-lam_init, scalar2=None, op0=ALU.add)
            P.dma("sp", gq2[:, 0, :], I["b_q_norm_g"][j].partition_broadcast(128), W=["gq2"])
            P.dma("sp", gq2[:, 1, :], I["k_norm_g"].partition_broadcast(128), W=["gq2"])
            P.op("dve", "tensor_reduce", ["gq2", "sc"], ["sc"], out=sc[:, 5:7], in_=gq2[:], axis=AX.X, op=ALU.max,
                 apply_absolute_value=True)
            P.op("dve", "tensor_tensor", ["sc"], ["sc"], out=sc[:, 7:8], in0=sc[:, 5:6], in1=sc[:, 6:7], op=ALU.mult)
            P.op("dve", "tensor_scalar", ["sc"], ["sc"], out=sc[:, 7:8], in0=sc[:, 7:8], scalar1=-(SCALE * 128.0), scalar2=None, op0=ALU.mult)
            P.dma("sp", subg[:], I["b_sub_norm_g"][j].partition_broadcast(128), W=["subg"])
            P.op("dve", "tensor_scalar", ["subg"], ["subg"], out=subg[:], in0=subg[:], scalar1=(1.0 - lam_init), scalar2=None, op0=ALU.mult)
            P.dma("pool", maskb[:], I["masks"].rearrange("a b k q -> k a b q"), W=["maskb"])
            P.op("dve", "memset", [], [("vt", 0), ("vt", 1)], vt[:].rearrange("p a r b d -> p (a r b) d")[:, :, 256:258], 1.0)

            def load_head(h):
                hb_ = h % 2
                for cc in range(2):
                    P.dma("sp", kts[:, hb_, cc, :, :], kview((2 * h + cc) * 128), W=[("kts", hb_)])
                for r in range(2):
                    P.dma("sp", vt[:, hb_, r, :, 0:256], vview(r, h), W=[("vt", hb_)])

            def load_item(h, i):
                qb = self.nxt("qt", 3)
                P.dma("sp", qt[:, qb, :, :],
                      S["qT"][2 * h * 128:(2 * h + 2) * 128, i * 128:(i + 1) * 128].rearrange("(c p) n -> p c n", p=128),
                      R=[("qTd", i)], W=[("qt", qb)])
                sgb = self.nxt("sgt", 3)
                P.dma("sp", sgt[:, sgb, :], S["sg"][i * 128:(i + 1) * 128, h * 256:(h + 1) * 256], R=[("sgd", i)], W=[("sgt", sgb)])
                return qb, sgb

            items = [(h, i) for h in range(H) for i in range(NS)]
            GL = []
            for n, (h, i) in enumerate(items):
                jj, pos = i // 2, i % 2
                blocks = []
                for m in range(jj):
                    blocks += [(0, 2 * m, None), (0, 2 * m + 1, None), (1, 2 * m, None), (1, 2 * m + 1, None)]
                if pos == 0:
                    blocks += [(0, 2 * jj, 0), (1, 2 * jj, 1)]
                else:
                    blocks += [(0, 2 * jj, None), (1, 2 * jj, None), (0, 2 * jj + 1, 0), (1, 2 * jj + 1, 1)]
                nb = len(blocks)
                for cc in range(2):
                    for g0 in range(0, nb, 4):
                        GL.append(dict(n=n, h=h, i=i, pos=pos, cc=cc, g0=g0, grp=blocks[g0:g0 + 4], nb=nb,
                                       first=(cc == 0 and g0 == 0), last=(cc == 1 and g0 + 4 >= nb)))
            load_head(0)
            if H > 1:
                load_head(1)
            item_bufs = {0: load_item(*items[0])}
            item_po = {}
            pending = [None]

            def start_item(n):
                h, i = items[n]
                if n + 1 < len(items):
                    item_bufs[n + 1] = load_item(*items[n + 1])
                item_po[n] = self.nxt("po", 2)

            def emit_qk(g):
                if g["first"]:
                    start_item(g["n"])
                hb_ = g["h"] % 2
                qb, sgb = item_bufs[g["n"]]
                mb = self.nxt("pm", 2)
                g["mb"] = mb
                cc = g["cc"]
                for bi, (r, lb, mk) in enumerate(g["grp"]):
                    P.op("pe", "matmul", [("kts", hb_), ("qt", qb)], [("pm", mb)], pm[:, mb, bi * 128:(bi + 1) * 128],
                         kts[:, hb_, cc, r, lb * 128:(lb + 1) * 128], qt[:, qb, cc, :], start=True, stop=(mk is None))
                    if mk is not None:
                        P.op("pe", "matmul", ["ident", "maskb"], [("pm", mb)], pm[:, mb, bi * 128:(bi + 1) * 128],
                             self.ident[:], maskb[:, g["pos"], mk, :], start=False, stop=True)

            def emit_exp_av(g):
                hb_ = g["h"] % 2
                mb = g["mb"]
                pob = item_po[g["n"]]
                cc, g0, nb = g["cc"], g["g0"], g["nb"]
                po = pg[:, 2 * pob + cc, 0:257]
                potok = ("pg", 2 * pob + cc)
                eb = self.nxt("et", 3)
                w_ = len(g["grp"]) * 128
                P.op("act", "activation", [("pm", mb), "sc"], [("et", eb)], out=et[:, eb, 0:w_], in_=pm[:, mb, 0:w_],
                     func=AF.Exp, scale=SCALE, bias=sc[:, 7:8])
                for bi, (r, lb, mk) in enumerate(g["grp"]):
                    P.op("pe", "matmul", [("et", eb), ("vt", hb_)], [potok], po, et[:, eb, bi * 128:(bi + 1) * 128],
                         vt[:, hb_, r, lb, 0:257], start=(g0 + bi == 0), stop=(g0 + bi == nb - 1))

            def combine(n):
                h, i = items[n]
                qb, sgb = item_bufs.pop(n)
                pob = item_po.pop(n)
                if pending[0] is not None:
                    pending[0]()
                    pending[0] = None
                o_ = self.nxt("ob", 3)
                po0, po1 = pg[:, 2 * pob, :], pg[:, 2 * pob + 1, :]
                t0k, t1k = ("pg", 2 * pob), ("pg", 2 * pob + 1)
                rk = ("rr", o_)
                P.op("dve", "reciprocal", [t0k], [rk], out=rr[:, o_, 0:1], in_=po0[:, 256:257])
                P.op("dve", "reciprocal", [t1k], [rk], out=rr[:, o_, 1:2], in_=po1[:, 256:257])
                P.op("dve", "tensor_tensor", [rk, "sc"], [rk], out=rr[:, o_, 2:3], in0=rr[:, o_, 1:2], in1=sc[:, 4:5], op=ALU.mult)
                P.op("dve", "tensor_scalar", [t1k, rk], [("ot", o_)], out=ot[:, o_, :], in0=po1[:, 0:256], scalar1=rr[:, o_, 2:3],
                     scalar2=None, op0=ALU.mult)
                P.op("dve", "scalar_tensor_tensor", [t0k, rk, ("ot", o_)], [("ot", o_)], out=ot[:, o_, :], in0=po0[:, 0:256],
                     scalar=rr[:, o_, 0:1], in1=ot[:, o_, :], op0=ALU.mult, op1=ALU.add)
                P.op("dve", "scalar_tensor_tensor", [("ot", o_), rk], [("o2", o_), rk], out=o2[:, o_, :], in0=ot[:, o_, :],
                     in1=ot[:, o_, :], scalar=1.0, op0=ALU.mult, op1=ALU.mult, accum_out=rr[:, o_, 3:4])
                self.rsqrt_small(rr[:, o_, 3:4], rr[:, o_, 3:4], 1.0 / 256, 0, 1, R=[rk], W=[rk])
                P.op("dve", "scalar_tensor_tensor", [("ot", o_), rk, "subg"], [("o2", o_)], out=o2[:, o_, :], in0=ot[:, o_, :],
                     scalar=rr[:, o_, 3:4], in1=subg[:], op0=ALU.mult, op1=ALU.mult)
                P.op("pool", "tensor_tensor", [("o2", o_), ("sgt", sgb)], [("ob", o_)], out=ob[:, o_, :], in0=o2[:, o_, :],
                     in1=sgt[:, sgb, :], op=ALU.mult)

                def tail(o_=o_, h=h, i=i):
                    self.transpose_into(lambda k: ob[:, o_, k * 128:(k + 1) * 128], [("ob", o_)], 2,
                                        lambda k0, n_: ots[:, o_, k0:k0 + n_, :], [("ots", o_)])
                    P.dma("sp", S["oT"][i][:, 2 * h:2 * h + 2, :], ots[:, o_, :, :], R=[("ots", o_)], W=[("oTd", i)])

                pending[0] = tail
                if i == NS - 1 and h + 2 < H:
                    load_head(h + 2)

            emit_qk(GL[0])
            for k, g in enumerate(GL):
                if k + 1 < len(GL):
                    emit_qk(GL[k + 1])
                emit_exp_av(g)
                if g["last"]:
                    combine(g["n"])
            if pending[0] is not None:
                pending[0]()
            self.barrier()

        self.out_proj(S["oT"], "oTd", KC, I["b_w_out"][j], src, dst, NS)


def chunk_lists(nchunks):
    g0 = [g for g in range(nchunks) if g % 4 in (0, 3)]
    g1 = [g for g in range(nchunks) if g % 4 in (1, 2)]
    return [g0, g1]


def make_masks(r):
    kk = np.arange(128)[:, None]
    qq = np.arange(128)[None, :]
    tri = np.where(kk <= qq, 0.0, -1e30).astype(np.float32)
    neg = np.full((128, 128), -1e30, np.float32)
    zero = np.zeros((128, 128), np.float32)
    if r == 0:
        return np.stack([np.stack([tri, neg]), np.stack([tri, zero])])
    return np.stack([np.stack([tri, zero]), np.stack([tri, neg])])


_NC_CACHE = {}


def get_nc(cfg, part):
    key = (cfg.D, cfg.NL, cfg.NLA, cfg.B, part)
    if key not in _NC_CACHE:
        _NC_CACHE[key] = Builder(cfg, part).build()
    return _NC_CACHE[key]


def run_model(inputs, cfg, B, fused=False):
    S_ = 2 * cfg.NL
    ncores = 2 * B
    D = cfg.D
    f32 = lambda a: np.ascontiguousarray(np.asarray(a, dtype=np.float32))
    x = f32(inputs["x"])
    p = f32(inputs["p"])
    cl = chunk_lists(S_ // 128)
    tok_idx = [np.concatenate([np.arange(g * 128, (g + 1) * 128) for g in cl[r]]) for r in range(2)]
    w = {k: f32(v) for k, v in inputs.items() if k not in ("x", "p")}
    NA = cfg.NA
    if fused:
        nc = get_nc(cfg, "ALL")
        in_maps = []
        for cid in range(ncores):
            b, r = cid // 2, cid % 2
            idx = np.concatenate([tok_idx[r], tok_idx[1 - r]])
            m = dict(w)
            m["x"] = np.ascontiguousarray(x[b][idx])
            m["p"] = np.ascontiguousarray(p[:, b][:, idx])
            m["masks"] = make_masks(r)
            in_maps.append(m)
        res = run_bass_kernel_spmd(nc, in_maps, core_ids=list(range(ncores)))
        outs = [r_["out"] for r_ in res.results]
    else:
        ncA = get_nc(cfg, "A")
        akeys = ["a_norm_g", "a_w_in", "a_ln_g", "a_ln_b", "a_w_s", "a_b_s", "a_w_out", "kv_norm_g", "w_kv", "k_norm_g"]
        in_maps = []
        for cid in range(ncores):
            b, r = cid // 2, cid % 2
            m = {k: w[k] for k in akeys}
            m["x"] = np.ascontiguousarray(x[b][tok_idx[r]])
            m["p"] = np.ascontiguousarray(p[:NA, b][:, tok_idx[r]])
            m["ple_w"] = np.ascontiguousarray(w["ple_w"][:NA])
            m["ple_gate_norm_g"] = np.ascontiguousarray(w["ple_gate_norm_g"][:NA])
            m["ple_gate_w"] = np.ascontiguousarray(w["ple_gate_w"][:NA])
            in_maps.append(m)
        resA = run_bass_kernel_spmd(ncA, in_maps, core_ids=list(range(ncores))).results
        ncB = get_nc(cfg, "B")
        bkeys = ["b_norm_g", "b_w_in", "b_q_norm_g", "b_lam_q1", "b_lam_k1", "b_lam_q2", "b_lam_k2", "b_sub_norm_g", "b_w_out", "k_norm_g"]
        in_maps = []
        for cid in range(ncores):
            b, r = cid // 2, cid % 2
            own, oth = 2 * b + r, 2 * b + 1 - r
            m = {k: w[k] for k in bkeys}
            m["h2"] = resA[cid]["h2"]
            m["ktf"] = np.ascontiguousarray(np.stack([resA[own]["kt"], resA[oth]["kt"]]))
            m["vf"] = np.ascontiguousarray(np.stack([resA[own]["vv"], resA[oth]["vv"]]))
            m["p"] = np.ascontiguousarray(p[NA:, b][:, tok_idx[r]])
            m["ple_w"] = np.ascontiguousarray(w["ple_w"][NA:])
            m["ple_gate_norm_g"] = np.ascontiguousarray(w["ple_gate_norm_g"][NA:])
            m["ple_gate_w"] = np.ascontiguousarray(w["ple_gate_w"][NA:])
            m["masks"] = make_masks(r)
            in_maps.append(m)
        resB = run_bass_kernel_spmd(ncB, in_maps, core_ids=list(range(ncores))).results
        outs = [r_["out"] for r_ in resB]
    out = np.empty((B, S_, D), np.float32)
    for cid in range(ncores):
        b, r = cid // 2, cid % 2
        out[b][tok_idx[r]] = outs[cid]
    return out


FUSED = True


def kernel(**inputs):
    if FUSED:
        cfg = Cfg(D=4096, NL=2048, NLA=4096)
    else:
        cfg = Cfg(D=4096, NL=2048)
    return run_model(inputs, cfg, B=4, fused=FUSED)
```

```python
import math
import numpy as np
import ml_dtypes
import concourse.bass as bass
import concourse.mybir as mybir
from concourse.bass_utils import run_bass_kernel_spmd

F32 = mybir.dt.float32
BF16 = mybir.dt.bfloat16
AF = mybir.ActivationFunctionType
ALU = mybir.AluOpType
AX = mybir.AxisListType

ENGS = ("pe", "act", "dve", "pool", "sp")


class _Op:
    __slots__ = ("eng", "fn", "deps", "is_dma", "signal", "dsem", "dval", "dprev", "eidx")

    def __init__(self, eng, fn, deps, is_dma):
        self.eng = eng
        self.fn = fn
        self.deps = deps
        self.is_dma = is_dma
        self.signal = False
        self.dsem = None
        self.dval = 0
        self.dprev = None
        self.eidx = 0


class Prog:
    NDMA = 8
    NSIG = 8

    def __init__(self, nc):
        self.nc = nc
        self.ops = []
        self.lastw = {}
        self.readers = {}

    def op(self, eng, name, R, W, *a, **kw):
        is_dma = kw.pop("_dma", False)
        return self.add(eng, lambda e: getattr(e, name)(*a, **kw), R, W, is_dma)

    def add(self, eng, fn, R=(), W=(), is_dma=False):
        idx = len(self.ops)
        deps = set()
        R = list(R) + ["PHASE"]
        for t in R:
            w = self.lastw.get(t)
            if w is not None:
                deps.add(w)
        for t in W:
            w = self.lastw.get(t)
            if w is not None:
                deps.add(w)
            for r in self.readers.get(t, ()):
                deps.add(r)
        self.ops.append(_Op(eng, fn, deps, is_dma))
        for t in R:
            self.readers.setdefault(t, []).append(idx)
        for t in W:
            self.lastw[t] = idx
            self.readers[t] = []
        return idx

    def dma(self, q, out, in_, R=(), W=(), slow=False):
        if slow:
            return self.add(q, lambda e: e.dma_start(out=out, in_=in_, allow_slow_non_contiguous=True), R, W, is_dma=True)
        return self.add(q, lambda e: e.dma_start(out=out, in_=in_), R, W, is_dma=True)

    def barrier(self, tile):
        idx = len(self.ops)
        deps = set(self.readers.get("PHASE", ()))
        w = self.lastw.get("PHASE")
        if w is not None:
            deps.add(w)
        self.ops.append(_Op("pool", lambda e: e.memset(tile, 0.0), deps, False))
        self.lastw = {"PHASE": idx}
        self.readers = {"PHASE": []}

    def emit(self, final_wait_eng="sp"):
        nc = self.nc
        ops = self.ops
        per = {e: [] for e in ENGS}
        for i, op in enumerate(ops):
            op.eidx = len(per[op.eng])
            per[op.eng].append(i)
        for e in ENGS:
            waited = {f: -1 for f in ENGS}
            for i in per[e]:
                op = ops[i]
                best = {}
                keep = []
                for d in op.deps:
                    dop = ops[d]
                    if dop.is_dma:
                        keep.append(d)
                        continue
                    if dop.eng == e and e == "pe":
                        continue
                    if dop.eidx <= waited[dop.eng]:
                        continue
                    if dop.eng not in best or ops[best[dop.eng]].eidx < dop.eidx:
                        best[dop.eng] = d
                for f, d in best.items():
                    keep.append(d)
                    waited[f] = ops[d].eidx
                    ops[d].signal = True
                op.deps = sorted(keep)
        cm = []

        def mk(name):
            g = nc.semaphore(name)
            h = g.__enter__()
            cm.append(g)
            return h

        sig = {e: [mk(f"sg_{e}{k}") for k in range(self.NSIG)] for e in ENGS}
        dsm = {e: [mk(f"dm_{e}{k}") for k in range(self.NDMA)] for e in ("sp", "pool", "act")}
        cnt_sig = {e: 0 for e in ENGS}
        cnt_dma = {e: 0 for e in ENGS}
        last_dma_on_sem = {}
        for i, op in enumerate(ops):
            if op.is_dma:
                n = cnt_dma[op.eng]
                cnt_dma[op.eng] += 1
                k = n % self.NDMA
                op.dsem = dsm[op.eng][k]
                op.dval = 16 * (n // self.NDMA + 1)
                op.dprev = last_dma_on_sem.get((op.eng, k))
                last_dma_on_sem[(op.eng, k)] = i
            elif op.signal:
                n = cnt_sig[op.eng]
                cnt_sig[op.eng] += 1
                op.dsem = sig[op.eng][n % self.NSIG]
                op.dval = n // self.NSIG + 1
        self.stats = {e: (len(per[e]), cnt_sig[e], cnt_dma[e]) for e in ENGS}
        engattr = {"pe": "tensor", "act": "scalar", "dve": "vector", "pool": "gpsimd", "sp": "sync"}
        all_dma = [i for i, op in enumerate(ops) if op.is_dma]
        with nc.Block() as block:
            for e in ENGS:
                lst = per[e]
                if not lst and e != final_wait_eng:
                    continue

                def body(eng, e=e, lst=lst):
                    done = {}

                    def wait(sem, val):
                        k = id(sem)
                        if done.get(k, 0) >= val:
                            return
                        done[k] = val
                        eng.wait_ge(sem, val)

                    for i in lst:
                        op = ops[i]
                        if op.is_dma and op.dprev is not None:
                            p = ops[op.dprev]
                            wait(p.dsem, p.dval)
                        for d in op.deps:
                            dop = ops[d]
                            wait(dop.dsem, dop.dval)
                        ins = op.fn(eng)
                        if op.is_dma:
                            ins.then_inc(op.dsem, 16)
                        elif op.signal:
                            ins.then_inc(op.dsem, 1)
                    if e == final_wait_eng:
                        lastper = {}
                        for i in all_dma:
                            op = ops[i]
                            lastper[id(op.dsem)] = (op.dsem, max(op.dval, lastper.get(id(op.dsem), (None, 0))[1]))
                        for sem, val in lastper.values():
                            wait(sem, val)

                getattr(block, engattr[e])(body)
        for g in reversed(cm):
            g.__exit__(None, None, None)


class Cfg:
    def __init__(self, D=4096, NL=2048, DEPTH=4, PLE=256, B=4, NLA=None):
        self.B = B
        self.NLA = NLA or NL
        self.NSA = self.NLA // 128
        self.D = D
        self.NL = NL
        self.NS = NL // 128
        self.E = 2 * D
        self.G = 16
        self.GD = self.E // 16
        self.H = D // 256
        self.PLE = PLE
        self.KC = D // 128
        self.NA = DEPTH // 2
        self.NB = DEPTH - self.NA
        self.DEPTH = DEPTH
        self.TP = min(1024, NL)
        self.NTP = self.TP // 128


RMS_EPS = 1e-6
LN_EPS = 1e-5


def lambda_init_fn(layer_idx):
    return 0.8 - 0.6 * math.exp(-0.3 * layer_idx)


class Builder:
    def __init__(self, cfg, part):
        self.c = cfg
        self.part = part
        self.nc = bass.Bass("TRN2", target_bir_lowering=False)
        self.P = Prog(self.nc)
        self.rot = {}

    def din(self, name, shape, dt=F32):
        return self.nc.dram_tensor(name, list(shape), dt, kind="ExternalInput").ap()

    def dout(self, name, shape, dt=F32):
        return self.nc.dram_tensor(name, list(shape), dt, kind="ExternalOutput").ap()

    def dscr(self, name, shape, dt=F32):
        return self.nc.dram_tensor(name, list(shape), dt, kind="Internal").ap()

    def sb(self, name, shape, dt):
        self._uid = getattr(self, "_uid", 0) + 1
        return self.nc.sbuf_tensor(f"{name}_{self._uid}", shape, dt)

    def nxt(self, key, n):
        v = self.rot.get(key, 0)
        self.rot[key] = v + 1
        return v % n

    def build(self):
        c = self.c
        nc = self.nc
        P = self.P
        D, E, NL, NS, H, KC = c.D, c.E, c.NL, c.NS, c.H, c.KC
        part = self.part
        I = {}
        if part in ("A", "ALL"):
            I["x"] = self.din("x", [c.NLA, D])
            I["a_norm_g"] = self.din("a_norm_g", [c.NA, D])
            I["a_w_in"] = self.din("a_w_in", [c.NA, D, 3 * E])
            I["a_ln_g"] = self.din("a_ln_g", [c.NA, E])
            I["a_ln_b"] = self.din("a_ln_b", [c.NA, E])
            I["a_w_s"] = self.din("a_w_s", [c.NA, c.G, 128, 128])
            I["a_b_s"] = self.din("a_b_s", [c.NA, c.G, 128])
            I["a_w_out"] = self.din("a_w_out", [c.NA, E, D])
            I["kv_norm_g"] = self.din("kv_norm_g", [D])
            I["w_kv"] = self.din("w_kv", [D, 2 * D])
        I["k_norm_g"] = self.din("k_norm_g", [128])
        if part in ("B", "ALL"):
            I["b_norm_g"] = self.din("b_norm_g", [c.NB, D])
            I["b_w_in"] = self.din("b_w_in", [c.NB, D, 2 * D])
            I["b_q_norm_g"] = self.din("b_q_norm_g", [c.NB, 128])
            for nm in ("b_lam_q1", "b_lam_k1", "b_lam_q2", "b_lam_k2"):
                I[nm] = self.din(nm, [c.NB, 128])
            I["b_sub_norm_g"] = self.din("b_sub_norm_g", [c.NB, 256])
            I["b_w_out"] = self.din("b_w_out", [c.NB, D, D])
            I["masks"] = self.din("masks", [2, 2, 128, 128])
        nl_layers = {"A": c.NA, "B": c.NB, "ALL": c.DEPTH}[part]
        I["p"] = self.din("p", [nl_layers, NL if part == "B" else c.NLA, c.PLE])
        I["ple_w"] = self.din("ple_w", [nl_layers, c.PLE, D])
        I["ple_gate_norm_g"] = self.din("ple_gate_norm_g", [nl_layers, D])
        I["ple_gate_w"] = self.din("ple_gate_w", [nl_layers, D, D])
        self.I = I
        ktf = vf = kt_loc = v_loc = h_in = None
        if part == "A":
            h_fin = self.dout("h2", [c.NLA, D])
            kt_loc = self.dout("kt", [2 * H * 128, c.NLA], BF16)
            v_loc = self.dout("vv", [c.NLA, H * 256], BF16)
        elif part == "B":
            h_in = self.din("h2", [NL, D])
            ktf = self.din("ktf", [2, 2 * H * 128, NL], BF16)
            vf = self.din("vf", [2, NL, H * 256], BF16)
            h_fin = self.dout("out", [NL, D])
        else:
            h_fin = self.dout("out", [NL, D])
            kt_loc = self.dscr("kt", [2 * H * 128, c.NLA], BF16)
            v_loc = self.dscr("vv", [c.NLA, H * 256], BF16)
        S = {}
        S["hA"] = self.dscr("hA", [c.NLA, D])
        S["hB"] = self.dscr("hB", [c.NLA, D])
        if part in ("A", "ALL"):
            S["gv"] = self.dscr("gv", [c.NLA, E])
            S["qq"] = self.dscr("qq", [c.NLA, E])
            S["yT"] = self.dscr("yT", [c.NSA, 128, E // 128, 128], BF16)
        if part in ("B", "ALL"):
            S["qT"] = self.dscr("qT", [2 * H * 128, NL], BF16)
            S["sg"] = self.dscr("sg", [NL, D])
            S["oT"] = self.dscr("oT", [NS, 128, KC, 128], BF16)
        self.S = S

        with (
            nc.psum_tensor("pg", [128, 4, 512], F32) as pg,
            nc.psum_tensor("pm", [128, 2, 512], F32) as pm,
            nc.psum_tensor("pt", [128, 2, 1024], BF16) as pt,
            self.sb("ident", [128, 128], BF16) as ident,
            self.sb("identf", [128, 128], F32) as identf,
            self.sb("cst", [128, 48], F32) as cst,
            self.sb("bar", [128, 2], F32) as bar,
        ):
            self.pg, self.pm, self.pt, self.ident, self.cst, self.bar = pg, pm, pt, ident, cst, bar
            P.op("pool", "memset", [], ["identf"], identf[:], 0.0)
            P.op("pool", "affine_select", ["identf"], ["identf"], out=identf[:], in_=identf[:], pattern=[[-1, 128]],
                 compare_op=ALU.not_equal, fill=1.0, base=0, channel_multiplier=1)
            P.op("dve", "tensor_copy", ["identf"], ["ident"], out=ident[:], in_=identf[:])
            P.op("pool", "memset", [], ["cst"], cst[:, 0:1], RMS_EPS)
            P.op("pool", "memset", [], ["cst"], cst[:, 1:2], LN_EPS)
            P.op("pool", "memset", [], ["cst"], cst[:, 2:48], -0.5)
            self.barrier()

            if part in ("A", "ALL"):
                cur = I["x"]
                for l in range(c.NA):
                    self.sgu_layer(l, cur, S["hA"], c.NSA)
                    dst = h_fin if (part == "A" and l == c.NA - 1) else S["hB"]
                    self.ple_layer(l, S["hA"], dst, c.NSA)
                    cur = dst
                self.kv_phase(cur, kt_loc, v_loc, c.NSA)
                if part == "ALL":
                    assert c.NLA == 2 * NL
                    kview = lambda row0: kt_loc[row0:row0 + 128, :].rearrange("p (r n) -> p r n", r=2)
                    vview = lambda r, h: v_loc[r * NL:(r + 1) * NL, h * 256:(h + 1) * 256].rearrange("(b p) n -> p b n", p=128)
            else:
                cur = h_in
                kview = lambda row0: ktf[:, row0:row0 + 128, :].rearrange("r p n -> p r n")
                vview = lambda r, h: vf[r][:, h * 256:(h + 1) * 256].rearrange("(b p) n -> p b n", p=128)
            if part in ("B", "ALL"):
                for j in range(c.NB):
                    li = (c.NA + j) if part == "ALL" else j
                    self.attn_layer(j, c.NA + j, cur, S["hA"], kview, vview)
                    dst = S["hB"] if j < c.NB - 1 else h_fin
                    self.ple_layer(li, S["hA"], dst, NS)
                    cur = dst
            P.emit()
        return nc

    def barrier(self):
        self.P.barrier(self.bar[:, 0:1])

    def rsqrt_small(self, out, in_, scale, eps_col, n, R, W):
        P, cst = self.P, self.cst
        P.op("dve", "tensor_scalar", R, W, out=out, in0=in_, scalar1=scale, scalar2=cst[:, eps_col:eps_col + 1],
             op0=ALU.mult, op1=ALU.add)
        P.op("pool", "tensor_tensor", W, W, out=out, in0=out, in1=cst[:, 2:2 + n], op=ALU.pow)

    def norm_to_xT(self, src, t0, nt, g_ap, XT, KCx, bufs):
        c, P = self.c, self.P
        D = c.D
        hs_, hb_, gB, ssq, rs = bufs
        nbuf = hs_.shape[1] if len(hs_.shape) == 3 else 1
        P.dma("sp", gB[:], g_ap.partition_broadcast(128), W=["gB"])
        for ti in range(nt):
            t = t0 + ti
            b = ti % nbuf
            hs = hs_[:, b, :] if nbuf > 1 else hs_[:]
            nhb = hb_.shape[1] if len(hb_.shape) == 3 else 1
            bb = ti % nhb
            hb = hb_[:, bb, :] if nhb > 1 else hb_[:]
            P.dma("sp", hs, src[t * 128:(t + 1) * 128, :], W=[("hs", b)])
            P.op("act", "activation", [("hs", b)], [("hb", bb), ("ssq", ti)], out=hb, in_=hs, func=AF.Square,
                 accum_out=ssq[:, ti:ti + 1])
            self.rsqrt_small(rs[:, ti:ti + 1], ssq[:, ti:ti + 1], 1.0 / D, 0, 1, R=[("ssq", ti)], W=[("rs", ti)])
            P.op("dve", "scalar_tensor_tensor", [("hs", b), ("rs", ti), "gB"], [("hb", bb)], out=hb, in0=hs,
                 scalar=rs[:, ti:ti + 1], in1=gB[:], op0=ALU.mult, op1=ALU.mult)
            self.transpose_into(lambda kc, hb=hb: hb[:, kc * 128:(kc + 1) * 128], [("hb", bb)], KCx,
                                lambda kc0, n, ti=ti: XT[:, ti, kc0:kc0 + n, :], [("XT", ti)])

    def transpose_into(self, src_fn, Rtok, nblk, dst_fn, Wtok):
        P = self.P
        pt, ident = self.pt, self.ident
        k0 = 0
        while k0 < nblk:
            n = min(8, nblk - k0)
            pb = self.nxt("pt", 2)
            for j in range(n):
                P.op("pe", "transpose", list(Rtok) + ["ident"], [("pt", pb)], pt[:, pb, j * 128:(j + 1) * 128],
                     src_fn(k0 + j), ident[:])
            dst = dst_fn(k0, n)
            srcv = pt[:, pb, 0:n * 128].rearrange("p (a b) -> p a b", b=128)
            if self.nxt("cpeng", 2) == 0:
                P.op("act", "activation", [("pt", pb)], Wtok, out=dst, in_=srcv, func=AF.Copy)
            else:
                P.op("dve", "tensor_copy", [("pt", pb)], Wtok, out=dst, in_=srcv)
            k0 += n

    def gemm(self, XT, nt, slabs, epilogue, Wr, prologue=None, tail_depth=1, before=None):
        P = self.P
        pg = self.pg
        NW = Wr.shape[1]
        parts = [(si, ki, loads) for si, kparts in enumerate(slabs) for ki, loads in enumerate(kparts)]

        def load(pi):
            si, ki, loads = parts[pi]
            wb = self.nxt("wr", NW)
            kcn = None
            for (wap, coff) in loads:
                ncols = wap.shape[1]
                kcn = wap.shape[0] // 128
                P.dma("pool", Wr[:, wb, 0:kcn, coff:coff + ncols], wap.rearrange("(kc p) n -> p kc n", p=128),
                      W=[("W", wb)])
            return wb, kcn

        loaded = {0: load(0)}
        if before is not None:
            before()
        pending = []
        banks = None
        for pi, (si, ki, loads) in enumerate(parts):
            nk = len(slabs[si])
            if pi + 1 < len(parts):
                loaded[pi + 1] = load(pi + 1)
            wb, kcn = loaded.pop(pi)
            if ki == 0:
                banks = [self.nxt("pg", 4) for _ in range(nt)] if nk > 1 else None
            for ti in range(nt):
                bk = banks[ti] if banks else self.nxt("pg", 4)
                if prologue is not None and ki == 0:
                    prologue(si, ti)
                for kc in range(kcn):
                    P.op("pe", "matmul", [("W", wb), ("XT", ti)], [("pg", bk)],
                         pg[:, bk, :], XT[:, ti, ki * kcn + kc, :], Wr[:, wb, kc, :],
                         start=(ki == 0 and kc == 0), stop=(ki == nk - 1 and kc == kcn - 1))
                while len(pending) >= tail_depth:
                    pending.pop(0)()
                if ki == nk - 1:
                    tl = epilogue(si, ti, pg[:, bk, :], ("pg", bk))
                    if tl is not None:
                        pending.append(tl)
        while pending:
            pending.pop(0)()

    def sgu_layer(self, l, src, dst, NS):
        c, P, nc, I, S = self.c, self.P, self.nc, self.I, self.S
        D, E, NL, KC, GD = c.D, c.E, c.NL, c.KC, c.GD
        NTP = c.NTP
        w_in = I["a_w_in"][l]
        nvs = E // 512
        nqs = E // 256
        with self.sb("mv", [128, NS, 4], F32) as mv:
            with (
                self.sb("XT", [128, NTP, KC, 128], BF16) as XT,
                self.sb("Wr", [128, 2, 32, 512], BF16) as Wr,
                self.sb("hs", [128, 2, D], F32) as hs,
                self.sb("hb", [128, D], BF16) as hb,
                self.sb("gB", [128, D], F32) as gB,
                self.sb("ssq", [128, NTP], F32) as ssq,
                self.sb("rs", [128, NTP], F32) as rs,
                self.sb("vsum", [128, NS, nvs], F32) as vsum,
                self.sb("vsq", [128, NS, nvs], F32) as vsq,
                self.sb("e1", [128, 3, 512], F32) as e1,
                self.sb("e2", [128, 2, 512], F32) as e2,
            ):
                nb = (hs, hb, gB, ssq, rs)
                P.op("dve", "memset", [], ["vsum"], vsum[:], 0.0)
                P.op("dve", "memset", [], ["vsq"], vsq[:], 0.0)
                for t0 in range(0, NS, NTP):
                    nrm = lambda t0=t0: self.norm_to_xT(src, t0, NTP, I["a_norm_g"][l], XT, KC, nb)
                    slabs = []
                    for s in range(nqs):
                        slabs.append([[(w_in[:, s * 256:(s + 1) * 256], 0),
                                       (w_in[:, 2 * E + s * 256:2 * E + (s + 1) * 256], 256)]])
                    for s in range(nvs):
                        slabs.append([[(w_in[:, E + s * 512:E + (s + 1) * 512], 0)]])

                    def epi(si, ti, ps, pstok, t0=t0):
                        t = t0 + ti
                        b = self.nxt("e1", 3)
                        if si < nqs:
                            P.op("act", "activation", [pstok], [("e1", b)], out=e1[:, b, 0:256], in_=ps[:, 0:256],
                                 func=AF.Gelu_apprx_tanh)
                            P.op("act", "activation", [pstok], [("e1", b)], out=e1[:, b, 256:512], in_=ps[:, 256:512],
                                 func=AF.Silu)
                            P.op("dve", "tensor_tensor", [("e1", b)], [("e1", b)], out=e1[:, b, 0:256], in0=e1[:, b, 0:256],
                                 in1=e1[:, b, 256:512], op=ALU.mult)
                            P.dma("sp", S["qq"][t * 128:(t + 1) * 128, si * 256:(si + 1) * 256], e1[:, b, 0:256],
                                  R=[("e1", b)], W=[("qq", t)])
                        else:
                            s = si - nqs
                            P.op("act", "activation", [pstok], [("e1", b), "vsum"], out=e1[:, b, :], in_=ps,
                                 func=AF.Gelu_apprx_tanh, accum_out=vsum[:, t, s:s + 1])
                            b2 = self.nxt("e2", 2)
                            P.op("dve", "scalar_tensor_tensor", [("e1", b)], [("e2", b2), "vsq"], out=e2[:, b2, :],
                                 in0=e1[:, b, :], in1=e1[:, b, :], scalar=1.0, op0=ALU.mult, op1=ALU.mult,
                                 accum_out=vsq[:, t, s:s + 1])
                            P.dma("sp", S["gv"][t * 128:(t + 1) * 128, s * 512:(s + 1) * 512], e1[:, b, :],
                                  R=[("e1", b)], W=[("gv", t)])

                    self.gemm(XT, NTP, slabs, epi, Wr, before=nrm)
                m0, m1, m2 = mv[:, :, 0], mv[:, :, 1], mv[:, :, 2]
                P.op("dve", "tensor_reduce", ["vsum"], ["mv"], out=m0, in_=vsum[:], axis=AX.X, op=ALU.add)
                P.op("dve", "tensor_reduce", ["vsq"], ["mv"], out=m1, in_=vsq[:], axis=AX.X, op=ALU.add)
                P.op("dve", "tensor_scalar", ["mv"], ["mv"], out=m0, in0=m0, scalar1=1.0 / E, scalar2=None, op0=ALU.mult)
                P.op("dve", "tensor_tensor", ["mv"], ["mv"], out=m2, in0=m0, in1=m0, op=ALU.mult)
                P.op("dve", "scalar_tensor_tensor", ["mv"], ["mv"], out=m1, in0=m1, scalar=1.0 / E, in1=m2,
                     op0=ALU.mult, op1=ALU.subtract)
                P.op("dve", "tensor_scalar", ["mv"], ["mv"], out=m1, in0=m1, scalar1=self.cst[:, 1:2], scalar2=None, op0=ALU.add)
                P.op("pool", "tensor_tensor", ["mv"], ["mv"], out=m1, in0=m1, in1=self.cst[:, 2:2 + NS], op=ALU.pow)
                P.op("dve", "scalar_tensor_tensor", ["mv"], ["mv"], out=mv[:, :, 3], in0=m0, scalar=-1.0, in1=m1,
                     op0=ALU.mult, op1=ALU.mult)
                self.barrier()

            EC = E // 128
            with (
                self.sb("lnG", [128, E], F32) as lnG,
                self.sb("lnB", [128, E], F32) as lnB,
                self.sb("gvs", [128, 2, E], F32) as gvs_,
                self.sb("vn", [128, E], BF16) as vn,
                self.sb("qs", [128, 4, 512], F32) as qs,
                self.sb("yb", [128, 3, 512], BF16) as yb,
                self.sb("yTs", [128, 2, EC, 128], BF16) as yTs,
                self.sb("wsf", [128, c.G, 128], F32) as wsf,
                self.sb("wsT", [128, c.G, 128], BF16) as wsT,
                self.sb("bsT", [128, c.G], F32) as bsT,
            ):
                P.dma("sp", lnG[:], I["a_ln_g"][l].partition_broadcast(128), W=["lnG"])
                P.dma("sp", lnB[:], I["a_ln_b"][l].partition_broadcast(128), W=["lnB"])
                P.dma("sp", wsf[:], I["a_w_s"][l].rearrange("g i j -> i g j"), W=["wsf"])
                P.op("pool", "affine_select", ["wsf"], ["wsf"], out=wsf[:], in_=wsf[:], pattern=[[0, c.G], [-1, 128]],
                     compare_op=ALU.is_ge, fill=0.0, base=0, channel_multiplier=1)
                P.op("dve", "tensor_copy", ["wsf"], [("yTs", 0)], out=yTs[:, 0, 0:c.G, :], in_=wsf[:])
                self.transpose_into(lambda g: yTs[:, 0, g, :], [("yTs", 0)], c.G,
                                    lambda g0, n: wsT[:, g0:g0 + n, :], ["wsT"])
                P.dma("sp", bsT[:], I["a_b_s"][l].rearrange("g i -> i g"), W=["bsT"], slow=True)
                ncol = min(GD, 512)
                npg = GD // ncol
                pend = None
                NCH = 4
                CW = E // NCH
                gtok = lambda gb: [("gvs", gb)] + [("gc", gb, ch) for ch in range(NCH)]
                P.dma("sp", gvs_[:, 0, :], S["gv"][0:128, :], R=[("gv", 0)], W=gtok(0))
                for t in range(NS):
                    gb = t % 2
                    gvs = gvs_[:, gb, :]
                    if t + 1 < NS:
                        P.dma("sp", gvs_[:, 1 - gb, :], S["gv"][(t + 1) * 128:(t + 2) * 128, :], R=[("gv", t + 1)],
                              W=gtok(1 - gb))
                    yt = t % 2

                    def ln_a(ch):
                        cs = slice(ch * CW, (ch + 1) * CW)
                        P.op("act", "activation", [("gvs", gb), "mv"], [("gc", gb, ch)], out=gvs[:, cs], in_=gvs[:, cs],
                             func=AF.Identity, scale=mv[:, t, 1:2], bias=mv[:, t, 3:4])
                        P.op("pool", "tensor_tensor", [("gc", gb, ch), "lnG"], [("gc", gb, ch)], out=gvs[:, cs], in0=gvs[:, cs],
                             in1=lnG[:, cs], op=ALU.mult)

                    def ln_b(ch):
                        cs = slice(ch * CW, (ch + 1) * CW)
                        P.op("dve", "tensor_tensor", [("gc", gb, ch), "lnB"], [("vn", ch)], out=vn[:, cs], in0=gvs[:, cs],
                             in1=lnB[:, cs], op=ALU.add)

                    def mix(ch):
                        nonlocal pend
                        for col0 in range(ch * CW, (ch + 1) * CW, ncol):
                            g = col0 // GD
                            mb = self.nxt("pm", 2)
                            P.op("pe", "matmul", ["wsT", ("vn", ch)], [("pm", mb)], self.pm[:, mb, 0:ncol], wsT[:, g, :],
                                 vn[:, col0:col0 + ncol], start=True, stop=True)
                            qb = self.nxt("qs", 4)
                            P.dma("sp", qs[:, qb, 0:ncol], S["qq"][t * 128:(t + 1) * 128, col0:col0 + ncol],
                                  R=[("qq", t)], W=[("qs", qb)])
                            ybb = self.nxt("yb", 3)
                            P.op("dve", "scalar_tensor_tensor", [("pm", mb), ("qs", qb), "bsT"], [("yb", ybb)],
                                 out=yb[:, ybb, 0:ncol], in0=self.pm[:, mb, 0:ncol], scalar=bsT[:, g:g + 1],
                                 in1=qs[:, qb, 0:ncol], op0=ALU.add, op1=ALU.mult)
                            ec0 = col0 // 128
                            if pend is not None:
                                pend()

                            def pend(ybb=ybb, ec0=ec0, yt=yt):
                                self.transpose_into(lambda k: yb[:, ybb, k * 128:(k + 1) * 128], [("yb", ybb)], ncol // 128,
                                                    lambda k0, n: yTs[:, yt, ec0 + k0:ec0 + k0 + n, :], [("yTs", yt)])

                    ln_a(0)
                    for ch in range(NCH):
                        if ch + 1 < NCH:
                            ln_a(ch + 1)
                        ln_b(ch)
                        mix(ch)
                    pend()
                    pend = None
                    P.dma("sp", S["yT"][t], yTs[:, yt, :, :], R=[("yTs", yt)], W=[("yTd", t)])
                self.barrier()

        self.out_proj(S["yT"], "yTd", E // 128, I["a_w_out"][l], src, dst, NS)

    def out_proj(self, xT_dram, xtok, KCx, w_ap, res_src, dst, NS):
        c, P, nc = self.c, self.P, self.nc
        D = c.D
        ntp = min(NS, (8 * 32) // KCx)
        kpc = min(32, KCx)
        nkp = KCx // kpc
        with (
            self.sb("XT", [128, ntp, KCx, 128], BF16) as XT,
            self.sb("Wr", [128, 2, 32, 512], BF16) as Wr,
            self.sb("hr", [128, 4, 512], F32) as hr,
        ):
            for t0 in range(0, NS, ntp):
                def ldx(t0=t0):
                    for ti in range(ntp):
                        P.dma("sp", XT[:, ti, :, :], xT_dram[t0 + ti], R=[(xtok, t0 + ti)], W=[("XT", ti)])
                slabs = []
                for s in range(D // 512):
                    slabs.append([[(w_ap[kp * kpc * 128:(kp + 1) * kpc * 128, s * 512:(s + 1) * 512], 0)] for kp in range(nkp)])

                hrb = {}

                def pro(si, ti, t0=t0):
                    t = t0 + ti
                    b = self.nxt("hr", 4)
                    hrb[(si, ti)] = b
                    P.dma("sp", hr[:, b, :], res_src[t * 128:(t + 1) * 128, si * 512:(si + 1) * 512], W=[("hr", b)])

                def epi(si, ti, ps, pstok, t0=t0):
                    t = t0 + ti
                    b = hrb.pop((si, ti))
                    P.op("dve", "tensor_tensor", [pstok, ("hr", b)], [("hr", b)], out=hr[:, b, :], in0=ps, in1=hr[:, b, :],
                         op=ALU.add)
                    P.dma("sp", dst[t * 128:(t + 1) * 128, si * 512:(si + 1) * 512], hr[:, b, :], R=[("hr", b)], W=[("hd", t)])

                self.gemm(XT, ntp, slabs, epi, Wr, prologue=pro, before=ldx)
            self.barrier()

    def ple_layer(self, li, src, dst, NS):
        c, P, nc, I = self.c, self.P, self.nc, self.I
        D, KC, NTP = c.D, c.KC, c.NTP
        PK = c.PLE // 128
        with (
            self.sb("XT", [128, NTP, KC, 128], BF16) as XT,
            self.sb("Wr", [128, 2, 32, 512], BF16) as Wr,
            self.sb("hs", [128, 2, D], F32) as hs,
            self.sb("hb", [128, D], BF16) as hb,
            self.sb("gB", [128, D], F32) as gB,
            self.sb("ssq", [128, NTP], F32) as ssq,
            self.sb("rs", [128, NTP], F32) as rs,
            self.sb("pf", [128, 2, c.PLE], F32) as pf,
            self.sb("pb", [128, 2, c.PLE], BF16) as pb,
            self.sb("pT", [128, NTP, PK, 128], BF16) as pT,
            self.sb("Wp", [128, 2, PK, 512], BF16) as Wp,
            self.sb("hr", [128, 3, 512], F32) as hr,
            self.sb("sgm", [128, 2, 512], F32) as sgm,
        ):
            nb = (hs, hb, gB, ssq, rs)
            for t0 in range(0, NS, NTP):
                nrm = lambda t0=t0: self.norm_to_xT(src, t0, NTP, I["ple_gate_norm_g"][li], XT, KC, nb)
                for ti in range(NTP):
                    t = t0 + ti
                    b = ti % 2
                    P.dma("sp", pf[:, b, :], I["p"][li][t * 128:(t + 1) * 128, :], W=[("pf", b)])
                    P.op("dve", "tensor_copy", [("pf", b)], [("pb", b)], out=pb[:, b, :], in_=pf[:, b, :])
                    self.transpose_into(lambda k, b=b: pb[:, b, k * 128:(k + 1) * 128], [("pb", b)], PK,
                                        lambda k0, n, ti=ti: pT[:, ti, k0:k0 + n, :], [("pT", ti)])
                slabs = [[[(I["ple_gate_w"][li][:, s * 512:(s + 1) * 512], 0)]] for s in range(D // 512)]
                state = {}
                hrb = {}

                def pro(si, ti, t0=t0):
                    t = t0 + ti
                    if ti == 0:
                        def ldwp(s):
                            wpb_ = self.nxt("wp", 2)
                            state[("wpb", s)] = wpb_
                            P.dma("pool", Wp[:, wpb_, :, :],
                                  I["ple_w"][li][:, s * 512:(s + 1) * 512].rearrange("(kc p) n -> p kc n", p=128),
                                  W=[("Wp", wpb_)])
                        if si == 0:
                            ldwp(0)
                        if si + 1 < D // 512:
                            ldwp(si + 1)
                    b = self.nxt("hr", 3)
                    hrb[(si, ti)] = b
                    P.dma("sp", hr[:, b, :], src[t * 128:(t + 1) * 128, si * 512:(si + 1) * 512], W=[("hr", b)])

                def epi(si, ti, ps, pstok, t0=t0):
                    t = t0 + ti
                    wpb = state[("wpb", si)]
                    sb = self.nxt("sgm", 2)
                    P.op("act", "activation", [pstok], [("sgm", sb)], out=sgm[:, sb, :], in_=ps, func=AF.Sigmoid)
                    mb = self.nxt("pm", 2)
                    for k in range(PK):
                        P.op("pe", "matmul", [("pT", ti), ("Wp", wpb)], [("pm", mb)], self.pm[:, mb, :], pT[:, ti, k, :],
                             Wp[:, wpb, k, :], start=(k == 0), stop=(k == PK - 1))
                    b = hrb.pop((si, ti))
                    P.op("dve", "tensor_tensor", [("sgm", sb), ("pm", mb)], [("sgm", sb)], out=sgm[:, sb, :], in0=sgm[:, sb, :],
                         in1=self.pm[:, mb, :], op=ALU.mult)
                    P.op("pool", "tensor_tensor", [("sgm", sb), ("hr", b)], [("hr", b)], out=hr[:, b, :], in0=sgm[:, sb, :],
                         in1=hr[:, b, :], op=ALU.add)
                    P.dma("sp", dst[t * 128:(t + 1) * 128, si * 512:(si + 1) * 512], hr[:, b, :], R=[("hr", b)], W=[("hd", t)])

                self.gemm(XT, NTP, slabs, epi, Wr, prologue=pro, before=nrm)
            self.barrier()

    def qk_epilogue(self, ps, pstok, gq, qT_dram, row0, t, bufs):
        P = self.P
        sq, qn, qts, ss4 = bufs
        b = self.nxt("sq", 4)
        v3 = lambda ap: ap.rearrange("p (a d) -> p a d", d=128)
        P.op("act", "activation", [pstok], [("sq", b)], out=sq[:, b, :], in_=ps, func=AF.Square)
        P.op("dve", "tensor_reduce", [("sq", b)], [("ss4", b)], out=ss4[:, b, :], in_=v3(sq[:, b, :]), axis=AX.X, op=ALU.add)
        self.rsqrt_small(ss4[:, b, :], ss4[:, b, :], 1.0 / 128, 0, 4, R=[("ss4", b)], W=[("ss4", b)])
        P.op("dve", "tensor_tensor", [pstok, ("ss4", b)], [("sq", b)], out=v3(sq[:, b, :]), in0=v3(ps),
             in1=ss4[:, b, :].unsqueeze(2).broadcast_to([128, 4, 128]), op=ALU.mult)
        P.op("pool", "tensor_tensor", [("sq", b), "gq"], [("qn", b)], out=v3(qn[:, b, :]), in0=v3(sq[:, b, :]),
             in1=gq.unsqueeze(1).broadcast_to([128, 4, 128]), op=ALU.mult)
        def tail():
            self.transpose_into(lambda k: qn[:, b, k * 128:(k + 1) * 128], [("qn", b)], 4,
                                lambda k0, n: qts[:, b, k0:k0 + n, :], [("qts", b)])
            P.dma("sp", qT_dram[row0:row0 + 512, t * 128:(t + 1) * 128].rearrange("(s p) n -> p s n", p=128), qts[:, b, :, :],
                  R=[("qts", b)], W=[("qTd", t)])
        return tail

    def kv_phase(self, src, kt_loc, v_loc, NS):
        c, P, nc, I = self.c, self.P, self.nc, self.I
        D, KC, NTP = c.D, c.KC, c.NTP
        with (
            self.sb("XT", [128, NTP, KC, 128], BF16) as XT,
            self.sb("Wr", [128, 2, 32, 512], BF16) as Wr,
            self.sb("hs", [128, 2, D], F32) as hs,
            self.sb("hb", [128, D], BF16) as hb,
            self.sb("gB", [128, D], F32) as gB,
            self.sb("ssq", [128, NTP], F32) as ssq,
            self.sb("rs", [128, NTP], F32) as rs,
            self.sb("sq", [128, 4, 512], F32) as sq,
            self.sb("qn", [128, 4, 512], BF16) as qn,
            self.sb("qts", [128, 4, 4, 128], BF16) as qts,
            self.sb("ss4", [128, 4, 4], F32) as ss4,
            self.sb("gk", [128, 128], F32) as gk,
            self.sb("vs", [128, 3, 512], BF16) as vs,
        ):
            nb = (hs, hb, gB, ssq, rs)
            P.dma("sp", gk[:], I["k_norm_g"].partition_broadcast(128), W=["gq"])
            nks = D // 512
            for t0 in range(0, NS, NTP):
                nrm = lambda t0=t0: self.norm_to_xT(src, t0, NTP, I["kv_norm_g"], XT, KC, nb)
                slabs = [[[(I["w_kv"][:, s * 512:(s + 1) * 512], 0)]] for s in range(2 * nks)]

                def epi(si, ti, ps, pstok, t0=t0):
                    t = t0 + ti
                    if si < nks:
                        return self.qk_epilogue(ps, pstok, gk[:], kt_loc, si * 512, t, (sq, qn, qts, ss4))
                    else:
                        b = self.nxt("vs", 3)
                        P.op("act", "activation", [pstok], [("vs", b)], out=vs[:, b, :], in_=ps, func=AF.Copy)
                        P.dma("sp", v_loc[t * 128:(t + 1) * 128, (si - nks) * 512:(si - nks + 1) * 512], vs[:, b, :],
                              R=[("vs", b)], W=[("vd", t)])

                self.gemm(XT, NTP, slabs, epi, Wr, tail_depth=2, before=nrm)
            self.barrier()

    def exchange(self, kt_loc, v_loc, ktf, vf):
        P = self.P
        groups = [[2 * i, 2 * i + 1] for i in range(self.c.B)]
        P.op("pool", "collective_compute", [], ["cc1"], "AllGather", ALU.bypass, replica_groups=groups, ins=[kt_loc],
             outs=[ktf], _dma=True)
        P.op("pool", "collective_compute", [], ["cc2"], "AllGather", ALU.bypass, replica_groups=groups, ins=[v_loc],
             outs=[vf], _dma=True)
        self.barrier()

    def attn_layer(self, j, layer_idx, src, dst, kview, vview):
        c, P, nc, I, S = self.c, self.P, self.nc, self.I, self.S
        D, NL, NS, KC, NTP, H = c.D, c.NL, c.NS, c.KC, c.NTP, c.H
        lam_init = lambda_init_fn(layer_idx)
        SCALE = 128 ** -0.5
        with (
            self.sb("XT", [128, NTP, KC, 128], BF16) as XT,
            self.sb("Wr", [128, 2, 32, 512], BF16) as Wr,
            self.sb("hs", [128, 2, D], F32) as hs,
            self.sb("hb", [128, D], BF16) as hb,
            self.sb("gB", [128, D], F32) as gB,
            self.sb("ssq", [128, NTP], F32) as ssq,
            self.sb("rs", [128, NTP], F32) as rs,
            self.sb("sq", [128, 4, 512], F32) as sq,
            self.sb("qn", [128, 4, 512], BF16) as qn,
            self.sb("qts", [128, 4, 4, 128], BF16) as qts,
            self.sb("ss4", [128, 4, 4], F32) as ss4,
            self.sb("gq", [128, 128], F32) as gq,
            self.sb("sgs", [128, 3, 512], F32) as sgs,
        ):
            nb = (hs, hb, gB, ssq, rs)
            P.dma("sp", gq[:], I["b_q_norm_g"][j].partition_broadcast(128), W=["gq"])
            nqs = D // 512
            for t0 in range(0, NS, NTP):
                nrm = lambda t0=t0: self.norm_to_xT(src, t0, NTP, I["b_norm_g"][j], XT, KC, nb)
                slabs = [[[(I["b_w_in"][j][:, s * 512:(s + 1) * 512], 0)]] for s in range(2 * nqs)]

                def epi(si, ti, ps, pstok, t0=t0):
                    t = t0 + ti
                    if si < nqs:
                        return self.qk_epilogue(ps, pstok, gq[:], S["qT"], si * 512, t, (sq, qn, qts, ss4))
                    else:
                        b = self.nxt("sgs", 3)
                        P.op("act", "activation", [pstok], [("sgs", b)], out=sgs[:, b, :], in_=ps, func=AF.Silu)
                        P.dma("sp", S["sg"][t * 128:(t + 1) * 128, (si - nqs) * 512:(si - nqs + 1) * 512], sgs[:, b, :],
                              R=[("sgs", b)], W=[("sgd", t)])

                self.gemm(XT, NTP, slabs, epi, Wr, tail_depth=2, before=nrm)
            self.barrier()

        pg, pm = self.pg, self.pm
        with (
            self.sb("kts", [128, 2, 2, 2, NL], BF16) as kts,
            self.sb("vt", [128, 2, 2, NS, 258], BF16) as vt,
            self.sb("qt", [128, 3, 2, 128], BF16) as qt,
            self.sb("sgt", [128, 3, 256], F32) as sgt,
            self.sb("et", [128, 3, 512], BF16) as et,
            self.sb("maskb", [128, 2, 2, 128], BF16) as maskb,
            self.sb("lv", [128, 4, 128], F32) as lv,
            self.sb("lj", [128, 128], F32) as lj,
            self.sb("sc", [128, 16], F32) as sc,
            self.sb("gq2", [128, 2, 128], F32) as gq2,
            self.sb("subg", [128, 256], F32) as subg,
            self.sb("rr", [128, 3, 8], F32) as rr,
            self.sb("ot", [128, 3, 256], F32) as ot,
            self.sb("o2", [128, 3, 256], F32) as o2,
            self.sb("ob", [128, 3, 256], BF16) as ob,
            self.sb("ots", [128, 3, 2, 128], BF16) as ots,
        ):
            for k, nm in enumerate(("b_lam_q1", "b_lam_k1", "b_lam_q2", "b_lam_k2")):
                P.dma("sp", lv[:, k, :], I[nm][j].partition_broadcast(128), W=[("lv", k)])
            P.op("dve", "memset", [], ["sc"], sc[:], 0.0)
            P.op("dve", "scalar_tensor_tensor", [("lv", 0), ("lv", 1), "sc"], ["lj", "sc"], out=lj[:], in0=lv[:, 0, :],
                 in1=lv[:, 1, :], scalar=1.0, op0=ALU.mult, op1=ALU.mult, accum_out=sc[:, 0:1])
            P.op("dve", "scalar_tensor_tensor", [("lv", 2), ("lv", 3), "sc", "lj"], ["lj", "sc"], out=lj[:], in0=lv[:, 2, :],
                 in1=lv[:, 3, :], scalar=1.0, op0=ALU.mult, op1=ALU.mult, accum_out=sc[:, 1:2])
            P.op("act", "activation", ["sc"], ["sc"], out=sc[:, 2:4], in_=sc[:, 0:2], func=AF.Exp)
            P.op("dve", "tensor_tensor", ["sc"], ["sc"], out=sc[:, 4:5], in0=sc[:, 3:4], in1=sc[:, 2:3], op=ALU.subtract)
            P.op("dve", "tensor_scalar", ["sc"], ["sc"], out=sc[:, 4:5], in0=sc[:, 4:5], scalar1=-lam_init, scalar2=None, op0=ALU.add)
            P.dma("sp", gq2[:, 0, :], I["b_q_norm_g"][j].partition_broadcast(128), W=["gq2"])
            P.dma("sp", gq2[:, 1, :], I["k_norm_g"].partition_broadcast(128), W=["gq2"])
            P.op("dve", "tensor_reduce", ["gq2", "sc"], ["sc"], out=sc[:, 5:7], in_=gq2[:], axis=AX.X, op=ALU.max,
                 apply_absolute_value=True)
            P.op("dve", "tensor_tensor", ["sc"], ["sc"], out=sc[:, 7:8], in0=sc[:, 5:6], in1=sc[:, 6:7], op=ALU.mult)
            P.op("dve", "tensor_scalar", ["sc"], ["sc"], out=sc[:, 7:8], in0=sc[:, 7:8], scalar1=-(SCALE * 128.0), scalar2=None, op0=ALU.mult)
            P.dma("sp", subg[:], I["b_sub_norm_g"][j].partition_broadcast(128), W=["subg"])
            P.op("dve", "tensor_scalar", ["subg"], ["subg"], out=subg[:], in0=subg[:], scalar1=(1.0 - lam_init), scalar2=None, op0=ALU.mult)
            P.dma("pool", maskb[:], I["masks"].rearrange("a b k q -> k a b q"), W=["maskb"])
            P.op("dve", "memset", [], [("vt", 0), ("vt", 1)], vt[:].rearrange("p a r b d -> p (a r b) d")[:, :, 256:258], 1.0)

            def load_head(h):
                hb_ = h % 2
                for cc in range(2):
                    P.dma("sp", kts[:, hb_, cc, :, :], kview((2 * h + cc) * 128), W=[("kts", hb_)])
                for r in range(2):
                    P.dma("sp", vt[:, hb_, r, :, 0:256], vview(r, h), W=[("vt", hb_)])

            def load_item(h, i):
                qb = self.nxt("qt", 3)
                P.dma("sp", qt[:, qb, :, :],
                      S["qT"][2 * h * 128:(2 * h + 2) * 128, i * 128:(i + 1) * 128].rearrange("(c p) n -> p c n", p=128),
                      R=[("qTd", i)], W=[("qt", qb)])
                sgb = self.nxt("sgt", 3)
                P.dma("sp", sgt[:, sgb, :], S["sg"][i * 128:(i + 1) * 128, h * 256:(h + 1) * 256], R=[("sgd", i)], W=[("sgt", sgb)])
                return qb, sgb

            items = [(h, i) for h in range(H) for i in range(NS)]
            GL = []
            for n, (h, i) in enumerate(items):
                jj, pos = i // 2, i % 2
                blocks = []
                for m in range(jj):
                    blocks += [(0, 2 * m, None), (0, 2 * m + 1, None), (1, 2 * m, None), (1, 2 * m + 1, None)]
                if pos == 0:
                    blocks += [(0, 2 * jj, 0), (1, 2 * jj, 1)]
                else:
                    blocks += [(0, 2 * jj, None), (1, 2 * jj, None), (0, 2 * jj + 1, 0), (1, 2 * jj + 1, 1)]
                nb = len(blocks)
                for cc in range(2):
                    for g0 in range(0, nb, 4):
                        GL.append(dict(n=n, h=h, i=i, pos=pos, cc=cc, g0=g0, grp=blocks[g0:g0 + 4], nb=nb,
                                       first=(cc == 0 and g0 == 0), last=(cc == 1 and g0 + 4 >= nb)))
            load_head(0)
            if H > 1:
                load_head(1)
            item_bufs = {0: load_item(*items[0])}
            item_po = {}
            pending = [None]

            def start_item(n):
                h, i = items[n]
                if n + 1 < len(items):
                    item_bufs[n + 1] = load_item(*items[n + 1])
                item_po[n] = self.nxt("po", 2)

            def emit_qk(g):
                if g["first"]:
                    start_item(g["n"])
                hb_ = g["h"] % 2
                qb, sgb = item_bufs[g["n"]]
                mb = self.nxt("pm", 2)
                g["mb"] = mb
                cc = g["cc"]
                for bi, (r, lb, mk) in enumerate(g["grp"]):
                    P.op("pe", "matmul", [("kts", hb_), ("qt", qb)], [("pm", mb)], pm[:, mb, bi * 128:(bi + 1) * 128],
                         kts[:, hb_, cc, r, lb * 128:(lb + 1) * 128], qt[:, qb, cc, :], start=True, stop=(mk is None))
                    if mk is not None:
                        P.op("pe", "matmul", ["ident", "maskb"], [("pm", mb)], pm[:, mb, bi * 128:(bi + 1) * 128],
                             self.ident[:], maskb[:, g["pos"], mk, :], start=False, stop=True)

            def emit_exp_av(g):
                hb_ = g["h"] % 2
                mb = g["mb"]
                pob = item_po[g["n"]]
                cc, g0, nb = g["cc"], g["g0"], g["nb"]
                po = pg[:, 2 * pob + cc, 0:257]
                potok = ("pg", 2 * pob + cc)
                eb = self.nxt("et", 3)
                w_ = len(g["grp"]) * 128
                P.op("act", "activation", [("pm", mb), "sc"], [("et", eb)], out=et[:, eb, 0:w_], in_=pm[:, mb, 0:w_],
                     func=AF.Exp, scale=SCALE, bias=sc[:, 7:8])
                for bi, (r, lb, mk) in enumerate(g["grp"]):
                    P.op("pe", "matmul", [("et", eb), ("vt", hb_)], [potok], po, et[:, eb, bi * 128:(bi + 1) * 128],
                         vt[:, hb_, r, lb, 0:257], start=(g0 + bi == 0), stop=(g0 + bi == nb - 1))

            def combine(n):
                h, i = items[n]
                qb, sgb = item_bufs.pop(n)
                pob = item_po.pop(n)
                if pending[0] is not None:
                    pending[0]()
                    pending[0] = None
                o_ = self.nxt("ob", 3)
                po0, po1 = pg[:, 2 * pob, :], pg[:, 2 * pob + 1, :]
                t0k, t1k = ("pg", 2 * pob), ("pg", 2 * pob + 1)
                rk = ("rr", o_)
                P.op("dve", "reciprocal", [t0k], [rk], out=rr[:, o_, 0:1], in_=po0[:, 256:257])
                P.op("dve", "reciprocal", [t1k], [rk], out=rr[:, o_, 1:2], in_=po1[:, 256:257])
                P.op("dve", "tensor_tensor", [rk, "sc"], [rk], out=rr[:, o_, 2:3], in0=rr[:, o_, 1:2], in1=sc[:, 4:5], op=ALU.mult)
                P.op("dve", "tensor_scalar", [t1k, rk], [("ot", o_)], out=ot[:, o_, :], in0=po1[:, 0:256], scalar1=rr[:, o_, 2:3],
                     scalar2=None, op0=ALU.mult)
                P.op("dve", "scalar_tensor_tensor", [t0k, rk, ("ot", o_)], [("ot", o_)], out=ot[:, o_, :], in0=po0[:, 0:256],
                     scalar=rr[:, o_, 0:1], in1=ot[:, o_, :], op0=ALU.mult, op1=ALU.add)
                P.op("dve", "scalar_tensor_tensor", [("ot", o_), rk], [("o2", o_), rk], out=o2[:, o_, :], in0=ot[:, o_, :],
                     in1=ot[:, o_, :], scalar=1.0, op0=ALU.mult, op1=ALU.mult, accum_out=rr[:, o_, 3:4])
                self.rsqrt_small(rr[:, o_, 3:4], rr[:, o_, 3:4], 1.0 / 256, 0, 1, R=[rk], W=[rk])
                P.op("dve", "scalar_tensor_tensor", [("ot", o_), rk, "subg"], [("o2", o_)], out=o2[:, o_, :], in0=ot[:, o_, :],
                     scalar=rr[:, o_, 3:4], in1=subg[:], op0=ALU.mult, op1=ALU.mult)
                P.op("pool", "tensor_tensor", [("o2", o_), ("sgt", sgb)], [("ob", o_)], out=ob[:, o_, :], in0=o2[:, o_, :],
                     in1=sgt[:, sgb, :], op=ALU.mult)

                def tail(o_=o_, h=h, i=i):
                    self.transpose_into(lambda k: ob[:, o_, k * 128:(k + 1) * 128], [("ob", o_)], 2,
                                        lambda k0, n_: ots[:, o_, k0:k0 + n_, :], [("ots", o_)])
                    P.dma("sp", S["oT"][i][:, 2 * h:2 * h + 2, :], ots[:, o_, :, :], R=[("ots", o_)], W=[("oTd", i)])

                pending[0] = tail
                if i == NS - 1 and h + 2 < H:
                    load_head(h + 2)

            emit_qk(GL[0])
            for k, g in enumerate(GL):
                if k + 1 < len(GL):
                    emit_qk(GL[k + 1])
                emit_exp_av(g)
                if g["last"]:
                    combine(g["n"])
            if pending[0] is not None:
                pending[0]()
            self.barrier()

        self.out_proj(S["oT"], "oTd", KC, I["b_w_out"][j], src, dst, NS)


def chunk_lists(nchunks):
    g0 = [g for g in range(nchunks) if g % 4 in (0, 3)]
    g1 = [g for g in range(nchunks) if g % 4 in (1, 2)]
    return [g0, g1]


def make_masks(r):
    kk = np.arange(128)[:, None]
    qq = np.arange(128)[None, :]
    tri = np.where(kk <= qq, 0.0, -1e30).astype(np.float32)
    neg = np.full((128, 128), -1e30, np.float32)
    zero = np.zeros((128, 128), np.float32)
    if r == 0:
        return np.stack([np.stack([tri, neg]), np.stack([tri, zero])])
    return np.stack([np.stack([tri, zero]), np.stack([tri, neg])])


_NC_CACHE = {}


def get_nc(cfg, part):
    key = (cfg.D, cfg.NL, cfg.NLA, cfg.B, part)
    if key not in _NC_CACHE:
        _NC_CACHE[key] = Builder(cfg, part).build()
    return _NC_CACHE[key]


def run_model(inputs, cfg, B, fused=False):
    S_ = 2 * cfg.NL
    ncores = 2 * B
    D = cfg.D
    f32 = lambda a: np.ascontiguousarray(np.asarray(a, dtype=np.float32))
    x = f32(inputs["x"])
    p = f32(inputs["p"])
    cl = chunk_lists(S_ // 128)
    tok_idx = [np.concatenate([np.arange(g * 128, (g + 1) * 128) for g in cl[r]]) for r in range(2)]
    w = {k: f32(v) for k, v in inputs.items() if k not in ("x", "p")}
    NA = cfg.NA
    if fused:
        nc = get_nc(cfg, "ALL")
        in_maps = []
        for cid in range(ncores):
            b, r = cid // 2, cid % 2
            idx = np.concatenate([tok_idx[r], tok_idx[1 - r]])
            m = dict(w)
            m["x"] = np.ascontiguousarray(x[b][idx])
            m["p"] = np.ascontiguousarray(p[:, b][:, idx])
            m["masks"] = make_masks(r)
            in_maps.append(m)
        res = run_bass_kernel_spmd(nc, in_maps, core_ids=list(range(ncores)))
        outs = [r_["out"] for r_ in res.results]
    else:
        ncA = get_nc(cfg, "A")
        akeys = ["a_norm_g", "a_w_in", "a_ln_g", "a_ln_b", "a_w_s", "a_b_s", "a_w_out", "kv_norm_g", "w_kv", "k_norm_g"]
        in_maps = []
        for cid in range(ncores):
            b, r = cid // 2, cid % 2
            m = {k: w[k] for k in akeys}
            m["x"] = np.ascontiguousarray(x[b][tok_idx[r]])
            m["p"] = np.ascontiguousarray(p[:NA, b][:, tok_idx[r]])
            m["ple_w"] = np.ascontiguousarray(w["ple_w"][:NA])
            m["ple_gate_norm_g"] = np.ascontiguousarray(w["ple_gate_norm_g"][:NA])
            m["ple_gate_w"] = np.ascontiguousarray(w["ple_gate_w"][:NA])
            in_maps.append(m)
        resA = run_bass_kernel_spmd(ncA, in_maps, core_ids=list(range(ncores))).results
        ncB = get_nc(cfg, "B")
        bkeys = ["b_norm_g", "b_w_in", "b_q_norm_g", "b_lam_q1", "b_lam_k1", "b_lam_q2", "b_lam_k2", "b_sub_norm_g", "b_w_out", "k_norm_g"]
        in_maps = []
        for cid in range(ncores):
            b, r = cid // 2, cid % 2
            own, oth = 2 * b + r, 2 * b + 1 - r
            m = {k: w[k] for k in bkeys}
            m["h2"] = resA[cid]["h2"]
            m["ktf"] = np.ascontiguousarray(np.stack([resA[own]["kt"], resA[oth]["kt"]]))
            m["vf"] = np.ascontiguousarray(np.stack([resA[own]["vv"], resA[oth]["vv"]]))
            m["p"] = np.ascontiguousarray(p[NA:, b][:, tok_idx[r]])
            m["ple_w"] = np.ascontiguousarray(w["ple_w"][NA:])
            m["ple_gate_norm_g"] = np.ascontiguousarray(w["ple_gate_norm_g"][NA:])
            m["ple_gate_w"] = np.ascontiguousarray(w["ple_gate_w"][NA:])
            m["masks"] = make_masks(r)
            in_maps.append(m)
        resB = run_bass_kernel_spmd(ncB, in_maps, core_ids=list(range(ncores))).results
        outs = [r_["out"] for r_ in resB]
    out = np.empty((B, S_, D), np.float32)
    for cid in range(ncores):
        b, r = cid // 2, cid % 2
        out[b][tok_idx[r]] = outs[cid]
    return out


FUSED = True


def kernel(**inputs):
    if FUSED:
        cfg = Cfg(D=4096, NL=2048, NLA=4096)
    else:
        cfg = Cfg(D=4096, NL=2048)
    return run_model(inputs, cfg, B=4, fused=FUSED)
```
